# Optimizing a Trainium2 kernel written in Bass

```python
import math
import jax, jax.numpy as jnp
from jax import lax
import numpy as np

D_MODEL = 1024
BATCH = 8
SEQ = 2048
DEPTH = 2

GRID_W = 64
CTX_LEN = 256
QBLOCK = 128
ROPE_BASE = 10000.0
EPS = 1e-6
RNN_WIDTH = D_MODEL
RNN_BLOCKS = 8
RNN_BLOCK_W = RNN_WIDTH // RNN_BLOCKS
CONV_W = 4
LRU_C = 8.0
MLA_HEADS = 16
MLA_Q_RANK = 3 * D_MODEL // 8
MLA_KV_RANK = D_MODEL // 4
MLA_NOPE = 64
MLA_ROPE = 32
MLA_V = 64
MLA_SCALE = (MLA_NOPE + MLA_ROPE) ** -0.5
DIFF_HEADS = 8
DIFF_HD = 64
DIFF_V = 2 * DIFF_HD
DIFF_SCALE = DIFF_HD ** -0.5
FFN_HIDDEN = -(-8 * D_MODEL // (3 * 256)) * 256
N_BRANCH = 3
IN_SPLITS = (RNN_WIDTH, RNN_WIDTH, MLA_Q_RANK, MLA_KV_RANK, MLA_ROPE,
             DIFF_HEADS * 2 * DIFF_HD, DIFF_HEADS * 2 * DIFF_HD, DIFF_HEADS * DIFF_V,
             N_BRANCH * D_MODEL)
IN_COLS = sum(IN_SPLITS)

kernel_name = "hybrid_rglru_mla_diffattn_dit_prefix"


def _rmsnorm(x, g):
    xf = x.astype(jnp.float32)
    y = xf * lax.rsqrt(jnp.mean(xf * xf, axis=-1, keepdims=True) + EPS)
    return (y * g.astype(jnp.float32)).astype(x.dtype)


def _split_cols(p, sizes):
    idx = np.cumsum(sizes)[:-1].tolist()
    return jnp.split(p, idx, axis=-1)


def _axial_rope_tables(T, rot_dim):
    rows = T // GRID_W
    row_ids = jnp.repeat(jnp.arange(rows, dtype=jnp.float32), GRID_W)
    col_ids = jnp.tile(jnp.arange(GRID_W, dtype=jnp.float32), rows)
    n = rot_dim // 4
    freqs = ROPE_BASE ** (-jnp.arange(n, dtype=jnp.float32) / n)
    ang = jnp.concatenate([row_ids[:, None] * freqs, col_ids[:, None] * freqs], axis=-1)
    return jnp.cos(ang), jnp.sin(ang)


def _apply_rope(x, cos, sin):
    shp = x.shape
    xf = x.astype(jnp.float32).reshape(shp[:-1] + (shp[-1] // 2, 2))
    bshape = (1, cos.shape[0]) + (1,) * (x.ndim - 3) + (cos.shape[1],)
    c = cos.reshape(bshape)
    s = sin.reshape(bshape)
    x1, x2 = xf[..., 0], xf[..., 1]
    out = jnp.stack([x1 * c - x2 * s, x1 * s + x2 * c], axis=-1)
    return out.reshape(shp).astype(x.dtype)


def _attention(q, k, v, mix_w, scale):
    B, Tq, M, H, dk = q.shape
    nb = Tq // QBLOCK
    qb = jnp.moveaxis(q.reshape(B, nb, QBLOCK, M, H, dk), 1, 0)
    w = mix_w.astype(jnp.float32)

    def block(qi):
        s = jnp.einsum('bqmhd,bkmhd->bmhqk', qi, k).astype(jnp.float32) * scale
        p = jnp.einsum('m,bmhqk->bhqk', w, jax.nn.softmax(s, axis=-1))
        return jnp.einsum('bhqk,bkhd->bqhd', p.astype(v.dtype), v)

    o = lax.map(block, qb)
    return jnp.moveaxis(o, 0, 1).reshape(B, Tq, H, v.shape[-1])


def _centred_dwconv(x, w, b):
    T = x.shape[1]
    left = CONV_W // 2
    xp = jnp.pad(x, ((0, 0), (left, CONV_W - 1 - left), (0, 0)))
    y = b + xp[:, 0:T] * w[0]
    for j in range(1, CONV_W):
        y = y + xp[:, j:j + T] * w[j]
    return y


def _lru_coeffs(x, wa, ba, wi, bi, lam):
    B, T, W = x.shape
    xb = x.reshape(B, T, RNN_BLOCKS, RNN_BLOCK_W)
    gr = jnp.einsum('btnc,rncd->rbtnd', xb, wa).reshape(2, B, T, W) + ba[:, None, None, :]
    gi = jnp.einsum('btnc,rncd->rbtnd', xb, wi).reshape(2, B, T, W) + bi[:, None, None, :]
    r = jax.nn.sigmoid(gr.astype(jnp.float32))
    i = jax.nn.sigmoid(gi.astype(jnp.float32))
    log_a = -LRU_C * r * jax.nn.softplus(-lam.astype(jnp.float32))[:, None, None, :]
    a = jnp.exp(log_a)
    b = jnp.sqrt(-jnp.expm1(2.0 * log_a)) * i * x.astype(jnp.float32)
    return a, b


def _scan_combine(e1, e2):
    a1, b1 = e1
    a2, b2 = e2
    return a1 * a2, a2 * b1 + b2


def _linear_scan(a, b, reverse, h0=None):
    a_cum, h = lax.associative_scan(_scan_combine, (a, b), reverse=reverse, axis=1)
    if h0 is not None:
        h = h + a_cum * h0[:, None, :]
    return h


def _mla_q(cq, lp, rope):
    B, T, _ = cq.shape
    q = (_rmsnorm(cq, lp['mla_qn_g']) @ lp['mla_w_uq']).reshape(B, T, MLA_HEADS, MLA_NOPE + MLA_ROPE)
    q = _rmsnorm(q, lp['mla_q_g'])
    if rope is not None:
        q = jnp.concatenate([q[..., :MLA_NOPE], _apply_rope(q[..., MLA_NOPE:], *rope)], axis=-1)
    return q[:, :, None]


def _mla_kv(ckv, kr, lp, rope):
    B, T, _ = ckv.shape
    kv = (_rmsnorm(ckv, lp['mla_kvn_g']) @ lp['mla_w_ukv']).reshape(B, T, MLA_HEADS, MLA_NOPE + MLA_V)
    k = jnp.concatenate([kv[..., :MLA_NOPE],
                         jnp.broadcast_to(kr[:, :, None, :], (B, T, MLA_HEADS, MLA_ROPE))], axis=-1)
    k = _rmsnorm(k, lp['mla_k_g'])
    if rope is not None:
        k = jnp.concatenate([k[..., :MLA_NOPE], _apply_rope(k[..., MLA_NOPE:], *rope)], axis=-1)
    return k[:, :, None], kv[..., MLA_NOPE:]


def _diff_qk(t, g, rope):
    B, T, _ = t.shape
    u = jnp.moveaxis(t.reshape(B, T, DIFF_HEADS, 2, DIFF_HD), 3, 2)
    u = _rmsnorm(u, g)
    if rope is not None:
        u = _apply_rope(u, *rope)
    return u


def _merge(y_a, rg, o_b, o_c, mg, lp, lam_init):
    B, T, _ = rg.shape
    br_a = (y_a.astype(rg.dtype) * jax.nn.gelu(rg)) @ lp['w_br_a']
    br_b = o_b.reshape(B, T, MLA_HEADS * MLA_V) @ lp['w_br_b']
    oc = _rmsnorm(o_c, lp['diff_subln_g']) * (1.0 - lam_init)
    br_c = oc.reshape(B, T, DIFF_HEADS * DIFF_V) @ lp['w_br_c']
    ga, gb, gc = jnp.split(jax.nn.sigmoid(mg), N_BRANCH, axis=-1)
    return (ga * br_a + gb * br_b + gc * br_c) @ lp['w_out']


def _ffn(h, lp):
    gate, up = jnp.split(h @ lp['w_ffn_in'], 2, axis=-1)
    return (jax.nn.silu(gate) * up) @ lp['w_ffn_out']


def _layer(x, xc, mod, mod_c, lp, rope_mla, rope_diff, lam_init, need_ctx):
    sh1, sc1, g1, sh2, sc2, g2 = jnp.split(mod, 6, axis=-1)
    csh1, csc1, cg1, csh2, csc2, cg2 = jnp.split(mod_c, 6, axis=-1)
    h = _rmsnorm(x, lp['norm1_g']) * (1.0 + sc1) + sh1
    hc = _rmsnorm(xc, lp['norm1_g']) * (1.0 + csc1) + csh1
    rx, rg, cq, ckv, kr, dq, dk, dv, mg = _split_cols(h @ lp['w_in'], IN_SPLITS)
    rxc, rgc, cqc, ckvc, krc, dqc, dkc, dvc, mgc = _split_cols(hc @ lp['w_in'], IN_SPLITS)
    B, T, _ = x.shape
    C = xc.shape[1]

    a, b = _lru_coeffs(_centred_dwconv(rx, lp['conv_w'], lp['conv_b']),
                       lp['lru_wa'], lp['lru_ba'], lp['lru_wi'], lp['lru_bi'], lp['lru_lambda'])
    ac, bc = _lru_coeffs(_centred_dwconv(rxc, lp['conv_w'], lp['conv_b']),
                         lp['lru_wa'], lp['lru_ba'], lp['lru_wi'], lp['lru_bi'], lp['lru_lambda'])
    hcf = _linear_scan(ac[0], bc[0], False)
    hcb = _linear_scan(ac[1], bc[1], True)
    y_a = _linear_scan(a[0], b[0], False, hcf[:, -1]) + _linear_scan(a[1], b[1], True, hcb[:, 0])

    k_l, v_l = _mla_kv(ckv, kr, lp, rope_mla)
    k_c, v_c = _mla_kv(ckvc, krc, lp, None)
    ones1 = jnp.ones((1,), jnp.float32)
    o_b = _attention(_mla_q(cq, lp, rope_mla), jnp.concatenate([k_l, k_c], axis=1),
                     jnp.concatenate([v_l, v_c], axis=1), ones1, MLA_SCALE)

    dl = lp['diff_lambda'].astype(jnp.float32)
    lam = jnp.exp(jnp.sum(dl[0] * dl[1])) - jnp.exp(jnp.sum(dl[2] * dl[3])) + lam_init
    wc = jnp.stack([jnp.ones((), jnp.float32), -lam])
    kd_l = _diff_qk(dk, lp['diff_k_g'], rope_diff)
    kd_c = _diff_qk(dkc, lp['diff_k_g'], None)
    vd_l = dv.reshape(B, T, DIFF_HEADS, DIFF_V)
    vd_c = dvc.reshape(B, C, DIFF_HEADS, DIFF_V)
    o_c = _attention(_diff_qk(dq, lp['diff_q_g'], rope_diff), jnp.concatenate([kd_l, kd_c], axis=1),
                     jnp.concatenate([vd_l, vd_c], axis=1), wc, DIFF_SCALE)

    x = x + g1 * _merge(y_a, rg, o_b, o_c, mg, lp, lam_init)
    x = x + g2 * _ffn(_rmsnorm(x, lp['norm2_g']) * (1.0 + sc2) + sh2, lp)

    if need_ctx:
        o_bc = _attention(_mla_q(cqc, lp, None), k_c, v_c, ones1, MLA_SCALE)
        o_cc = _attention(_diff_qk(dqc, lp['diff_q_g'], None), kd_c, vd_c, wc, DIFF_SCALE)
        xc = xc + cg1 * _merge(hcf + hcb, rgc, o_bc, o_cc, mgc, lp, lam_init)
        xc = xc + cg2 * _ffn(_rmsnorm(xc, lp['norm2_g']) * (1.0 + csc2) + csh2, lp)
    return x, xc


def setup_inputs(seed: int = 0) -> dict:
    key = jax.random.key(seed)
    ks = iter(jax.random.split(key, 40))
    L = DEPTH
    f32 = jnp.float32

    def nrm(shape, fan_in):
        return jax.random.normal(next(ks), shape, f32) * fan_in ** -0.5

    def gain(shape):
        return 1.0 + 0.02 * jax.random.normal(next(ks), shape, f32)

    def small(shape):
        return 0.01 * jax.random.normal(next(ks), shape, f32)

    u = jax.random.uniform(next(ks), (L, 2, RNN_WIDTH), f32, 0.9, 0.999)
    a0 = u ** (1.0 / LRU_C)
    lru_lambda = jnp.log(a0) - jnp.log1p(-a0)
    return {
        'x': jax.random.normal(next(ks), (BATCH, SEQ, D_MODEL), f32),
        'c': jax.random.normal(next(ks), (BATCH, D_MODEL), f32),
        'ctx': jax.random.normal(next(ks), (BATCH, CTX_LEN, D_MODEL), f32),
        'c_ctx': jax.random.normal(next(ks), (D_MODEL,), f32),
        'w_mod': nrm((L, D_MODEL, 6 * D_MODEL), D_MODEL),
        'b_mod': small((L, 6 * D_MODEL)),
        'norm1_g': gain((L, D_MODEL)),
        'norm2_g': gain((L, D_MODEL)),
        'w_in': nrm((L, D_MODEL, IN_COLS), D_MODEL),
        'conv_w': nrm((L, CONV_W, RNN_WIDTH), CONV_W),
        'conv_b': small((L, RNN_WIDTH)),
        'lru_wa': nrm((L, 2, RNN_BLOCKS, RNN_BLOCK_W, RNN_BLOCK_W), RNN_BLOCK_W),
        'lru_ba': small((L, 2, RNN_WIDTH)),
        'lru_wi': nrm((L, 2, RNN_BLOCKS, RNN_BLOCK_W, RNN_BLOCK_W), RNN_BLOCK_W),
        'lru_bi': small((L, 2, RNN_WIDTH)),
        'lru_lambda': lru_lambda,
        'mla_qn_g': gain((L, MLA_Q_RANK)),
        'mla_w_uq': nrm((L, MLA_Q_RANK, MLA_HEADS * (MLA_NOPE + MLA_ROPE)), MLA_Q_RANK),
        'mla_kvn_g': gain((L, MLA_KV_RANK)),
        'mla_w_ukv': nrm((L, MLA_KV_RANK, MLA_HEADS * (MLA_NOPE + MLA_V)), MLA_KV_RANK),
        'mla_q_g': gain((L, MLA_NOPE + MLA_ROPE)),
        'mla_k_g': gain((L, MLA_NOPE + MLA_ROPE)),
        'diff_q_g': gain((L, DIFF_HD)),
        'diff_k_g': gain((L, DIFF_HD)),
        'diff_lambda': 0.1 * jax.random.normal(next(ks), (L, 4, DIFF_HD), f32),
        'diff_subln_g': gain((L, DIFF_V)),
        'w_br_a': nrm((L, RNN_WIDTH, D_MODEL), RNN_WIDTH),
        'w_br_b': nrm((L, MLA_HEADS * MLA_V, D_MODEL), MLA_HEADS * MLA_V),
        'w_br_c': nrm((L, DIFF_HEADS * DIFF_V, D_MODEL), DIFF_HEADS * DIFF_V),
        'w_out': nrm((L, D_MODEL, D_MODEL), D_MODEL),
        'w_ffn_in': nrm((L, D_MODEL, 2 * FFN_HIDDEN), D_MODEL),
        'w_ffn_out': nrm((L, FFN_HIDDEN, D_MODEL), FFN_HIDDEN),
    }


def reference(x, c, ctx, c_ctx, w_mod, b_mod, norm1_g, norm2_g, w_in, conv_w, conv_b,
              lru_wa, lru_ba, lru_wi, lru_bi, lru_lambda, mla_qn_g, mla_w_uq, mla_kvn_g,
              mla_w_ukv, mla_q_g, mla_k_g, diff_q_g, diff_k_g, diff_lambda, diff_subln_g,
              w_br_a, w_br_b, w_br_c, w_out, w_ffn_in, w_ffn_out):
    T = x.shape[1]
    rope_mla = _axial_rope_tables(T, MLA_ROPE)
    rope_diff = _axial_rope_tables(T, DIFF_HD)
    xc = ctx
    sc = jax.nn.silu(c)
    scc = jax.nn.silu(c_ctx)
    for l in range(DEPTH):
        lp = dict(norm1_g=norm1_g[l], norm2_g=norm2_g[l], w_in=w_in[l], conv_w=conv_w[l],
                  conv_b=conv_b[l], lru_wa=lru_wa[l], lru_ba=lru_ba[l], lru_wi=lru_wi[l],
                  lru_bi=lru_bi[l], lru_lambda=lru_lambda[l], mla_qn_g=mla_qn_g[l],
                  mla_w_uq=mla_w_uq[l], mla_kvn_g=mla_kvn_g[l], mla_w_ukv=mla_w_ukv[l],
                  mla_q_g=mla_q_g[l], mla_k_g=mla_k_g[l], diff_q_g=diff_q_g[l],
                  diff_k_g=diff_k_g[l], diff_lambda=diff_lambda[l], diff_subln_g=diff_subln_g[l],
                  w_br_a=w_br_a[l], w_br_b=w_br_b[l], w_br_c=w_br_c[l], w_out=w_out[l],
                  w_ffn_in=w_ffn_in[l], w_ffn_out=w_ffn_out[l])
        mod = (sc @ w_mod[l] + b_mod[l])[:, None, :]
        mod_c = scc @ w_mod[l] + b_mod[l]
        lam_init = 0.8 - 0.6 * math.exp(-0.3 * l)
        x, xc = _layer(x, xc, mod, mod_c, lp, rope_mla, rope_diff, lam_init, l < DEPTH - 1)
    return x
```

```python
import contextlib
import numpy as np
import concourse.bass as bass
import concourse.mybir as mybir
from concourse.bass_utils import run_bass_kernel_spmd

F32 = mybir.dt.float32
BF16 = mybir.dt.bfloat16
AF = mybir.ActivationFunctionType
ALU = mybir.AluOpType

D = 1024
T = 2048
C = 256
NT = T + C
KC = 8
DEPTH = 2
EPS = 1e-6
FFH = 2816
TBLK = [(0, 512), (512, 512), (1024, 512), (1536, 512), (2048, 256)]
MLA_FILL = 2
DIFF_FILL = 1
MLA_SCALE = 96 ** -0.5
DIFF_SCALE = 64 ** -0.5

O_RX, O_RG, O_CQ, O_CKV, O_KR, O_DQ, O_DK, O_MG, O_DV = 0, 1024, 2048, 2432, 2688, 2752, 4800, 6848, 9920
NEXT = 10944
V_BMOD, V_N1G, V_N2G, V_CONVW, V_CONVB, V_BA, V_BI, V_LAM, V_QNG, V_KVNG = 0, 48, 56, 64, 96, 104, 120, 136, 152, 155
V_GQ, V_GQSW, V_GK, V_GKSW, V_GDQ, V_GDQSW, V_GDK, V_GDKSW = 157, 158, 159, 160, 161, 162, 163, 164
NV = 165

SELF_SYNC = True


def _dsz(dt):
    return 2 if dt == BF16 else 4


def _box(ap):
    t = ap.tensor
    dims = ap.ap
    off = int(ap.offset)
    space = str(ap.space)
    z = _dsz(ap.dtype)
    if space in ("SB", "PSUM"):
        pstep, pcnt = dims[0]
        if pstep == 0:
            p0, f = 0, off
        else:
            p0 = off // pstep
            f = off - p0 * pstep
        lo = hi = f
        for st, cnt in dims[1:]:
            ext = st * (cnt - 1)
            if ext >= 0:
                hi += ext
            else:
                lo += ext
        return (t.name, p0, p0 + pcnt, lo * z, (hi + 1) * z)
    lo = hi = off
    for st, cnt in dims:
        ext = st * (cnt - 1)
        if ext >= 0:
            hi += ext
        else:
            lo += ext
    return (t.name, 0, 1, lo * z, (hi + 1) * z)


def _overlap(a, b):
    return a[1] < b[2] and b[1] < a[2] and a[3] < b[4] and b[3] < a[4]


def _contains(a, b):
    return a[1] <= b[1] and b[2] <= a[2] and a[3] <= b[3] and b[4] <= a[4]


class Op:
    __slots__ = ("eng", "fn", "waits", "semkey", "seq", "snap", "has_dep", "val", "is_dma")

    def __init__(self, eng, fn):
        self.eng = eng
        self.fn = fn
        self.waits = []
        self.semkey = None
        self.seq = 0
        self.snap = None
        self.has_dep = False
        self.val = 0
        self.is_dma = False


class Prog:
    ENGS = ("pe", "act", "dve", "pool", "sp")

    def __init__(self, nc, n_dma_sems=40):
        self.nc = nc
        self.ops = {e: [] for e in self.ENGS}
        self.known = {e: {} for e in self.ENGS}
        self.recs = {}
        self.n_dma_sems = n_dma_sems
        self.dma_rr = 0
        self.dma_last = {}

    def _need(self, op, prod):
        if prod is op:
            return
        e = op.eng
        kn = self.known[e]
        if kn.get(prod.semkey, -1) >= prod.seq:
            return
        if (not prod.is_dma) and prod.eng == e:
            if e == "pe" or not SELF_SYNC:
                return
        op.waits.append(prod)
        prod.has_dep = True
        for k, v in prod.snap.items():
            if kn.get(k, -1) < v:
                kn[k] = v
        if kn.get(prod.semkey, -1) < prod.seq:
            kn[prod.semkey] = prod.seq

    def _track(self, op, reads, writes):
        rb = [_box(a) for a in reads]
        wb = [_box(a) for a in writes]
        for b in rb:
            for (ob, oop, ow) in self.recs.get(b[0], ()):
                if ow and _overlap(b, ob):
                    self._need(op, oop)
        for b in wb:
            for (ob, oop, ow) in self.recs.get(b[0], ()):
                if _overlap(b, ob):
                    self._need(op, oop)
        for b in wb:
            lst = self.recs.setdefault(b[0], [])
            lst[:] = [r for r in lst if not _contains(b, r[0])]
            lst.append((b, op, True))
        for b in rb:
            lst = self.recs.setdefault(b[0], [])
            if not op.is_dma:
                lst[:] = [r for r in lst if r[2] or r[1].is_dma or r[1].eng != op.eng
                          or not _contains(b, r[0])]
            lst.append((b, op, False))

    def op(self, eng, fn, reads=(), writes=()):
        o = Op(eng, fn)
        o.semkey = eng
        o.seq = len(self.ops[eng])
        self._track(o, reads, writes)
        o.snap = dict(self.known[eng])
        self.ops[eng].append(o)
        return o

    def dma(self, queue, out, in_, **kw):
        def fn(engobj):
            return engobj.dma_start(out=out, in_=in_, **kw)
        o = Op(queue, fn)
        o.is_dma = True
        idx = self.dma_rr
        self.dma_rr = (self.dma_rr + 1) % self.n_dma_sems
        o.semkey = ("dma", idx)
        prev = self.dma_last.get(idx)
        if prev is not None:
            o.seq = prev.seq + 1
            kn = self.known[queue]
            if kn.get(prev.semkey, -1) < prev.seq:
                o.waits.append(prev)
                for k, v in prev.snap.items():
                    if kn.get(k, -1) < v:
                        kn[k] = v
                kn[prev.semkey] = prev.seq
        self.dma_last[idx] = o
        self._track(o, [in_], [out])
        o.snap = dict(self.known[queue])
        o.has_dep = True
        self.ops[queue].append(o)
        return o

    def emit(self, final_wait_ops=()):
        nc = self.nc
        with contextlib.ExitStack() as st:
            sems = {}
            for e in ("pe", "act", "dve", "pool"):
                sems[e] = st.enter_context(nc.semaphore("s_" + e))
            for i in range(self.n_dma_sems):
                sems[("dma", i)] = st.enter_context(nc.semaphore("s_dma%d" % i))
            for w in final_wait_ops:
                w.has_dep = True
            for e, lst in self.ops.items():
                c = 0
                for o in lst:
                    if o.is_dma:
                        o.val = 16 * (o.seq + 1)
                    elif o.has_dep:
                        c += 1
                        o.val = c
            block = st.enter_context(nc.Block())

            def run(e):
                def body(engobj):
                    for o in self.ops[e]:
                        for w in o.waits:
                            engobj.wait_ge(sems[w.semkey], w.val)
                        ins = o.fn(engobj)
                        if o.is_dma:
                            ins.then_inc(sems[o.semkey], 16)
                        elif o.has_dep:
                            ins.then_inc(sems[o.semkey], 1)
                    if e == "sp":
                        for w in final_wait_ops:
                            engobj.wait_ge(sems[w.semkey], w.val)
                return body

            block.tensor(run("pe"))
            block.scalar(run("act"))
            block.vector(run("dve"))
            block.gpsimd(run("pool"))
            block.sync(run("sp"))


def _isap(x):
    return hasattr(x, "tensor") and hasattr(x, "ap")


def build(n_layers=DEPTH, debug=False, force_ctx=False, stop_after=None):
    nc = bass.Bass("TRN2", target_bir_lowering=False)
    P = Prog(nc)
    st = contextlib.ExitStack()

    def din(name, shape, dt=F32):
        return nc.dram_tensor(name, list(shape), dt, kind="ExternalInput").ap()

    def dscr(name, shape, dt=F32):
        return nc.dram_tensor(name, list(shape), dt, kind="ExternalOutput" if debug else "Internal").ap()

    def sb(name, shape, dt=F32):
        return st.enter_context(nc.sbuf_tensor("sb_" + name, list(shape), dt))

    def ACT(out, in_, func, bias=None, scale=None, accum=None):
        kw = {}
        rd = [in_]
        if bias is not None:
            kw["bias"] = bias
            if _isap(bias):
                rd.append(bias)
        if scale is not None:
            kw["scale"] = scale
            if _isap(scale):
                rd.append(scale)
        wr = [out]
        if accum is not None:
            kw["accum_out"] = accum
            wr.append(accum)
        return P.op("act", lambda e: e.activation(out=out, in_=in_, func=func, **kw), reads=rd, writes=wr)

    def TT(eng, out, in0, in1, op):
        return P.op(eng, lambda e: e.tensor_tensor(out=out, in0=in0, in1=in1, op=op), reads=[in0, in1], writes=[out])

    def STT(eng, out, in0, scalar, in1, op0, op1):
        eng = "dve"
        rd = [in0, in1] + ([scalar] if _isap(scalar) else [])
        return P.op(eng, lambda e: e.scalar_tensor_tensor(out=out, in0=in0, scalar=scalar, in1=in1, op0=op0, op1=op1),
                    reads=rd, writes=[out])

    def TS(eng, out, in0, s1, op0, s2=None, op1=None):
        rd = [in0] + [s for s in (s1, s2) if _isap(s)]
        if op1 is None:
            return P.op(eng, lambda e: e.tensor_scalar(out=out, in0=in0, scalar1=s1, scalar2=None, op0=op0), reads=rd, writes=[out])
        return P.op(eng, lambda e: e.tensor_scalar(out=out, in0=in0, scalar1=s1, scalar2=s2, op0=op0, op1=op1), reads=rd, writes=[out])

    def CP(eng, out, in_):
        if eng == "act":
            return ACT(out, in_, AF.Copy)
        return P.op(eng, lambda e: e.tensor_copy(out=out, in_=in_), reads=[in_], writes=[out])

    def MSET(eng, ap, val):
        return P.op(eng, lambda e: e.memset(ap, val), writes=[ap])

    def RECIP(out, in_):
        return P.op("dve", lambda e: e.reciprocal(out=out, in_=in_), reads=[in_], writes=[out])

    def RSTD(out, ss, scale, eps):
        if scale == 1.0:
            ACT(out, ss, AF.Sqrt, bias=float(eps))
        else:
            ACT(out, ss, AF.Sqrt, bias=float(eps), scale=float(scale))
        RECIP(out, out)

    def MM(out, lhsT, rhs, start, stop, **kw):
        return P.op("pe", lambda e: e.matmul(out, lhsT=lhsT, rhs=rhs, start=start, stop=stop, **kw),
                    reads=[lhsT, rhs], writes=[out])

    def TR(out, in_, ident):
        return P.op("pe", lambda e: e.transpose(out, in_, ident), reads=[in_, ident], writes=[out])

    def SCAN(out, a, b, init):
        rd = [a, b] + ([init] if _isap(init) else [])
        return P.op("dve", lambda e: e.tensor_tensor_scan(out=out, data0=a, data1=b, initial=init, op0=ALU.mult, op1=ALU.add),
                    reads=rd, writes=[out])

    xT_in = din("xT", [D, NT])
    cvec_in = din("cvec", [128, KC, 2])
    ident_in = din("ident", [128, 128])
    masks_in = din("masks", [128, 3, 128])
    ropeM_in = din("ropeM", [2, 128, NT])
    ropeD_in = din("ropeD", [2, 128, NT])
    W = []
    for l in range(n_layers):
        W.append(dict(
            wmod=din(f"wmod{l}", [D, 6 * D]), win=din(f"win{l}", [D, NEXT]), wuq=din(f"wuq{l}", [384, 2048]),
            wukvk=din(f"wukvk{l}", [256, 1024]), wukvv=din(f"wukvv{l}", [256, 1024]),
            lru=din(f"lru{l}", [128, 32, 128]), wbra=din(f"wbra{l}", [D, D]), wbrb=din(f"wbrb{l}", [D, D]),
            wbrc=din(f"wbrc{l}", [D, D]), wout=din(f"wout{l}", [D, D]), wffi=din(f"wffi{l}", [D, 2 * FFH]),
            wffo=din(f"wffo{l}", [FFH, D]), vecs=din(f"vecs{l}", [128, NV]), dlam=din(f"dlam{l}", [1, 256]),
            subg=din(f"subg{l}", [1, 128])))
    outT = nc.dram_tensor("outT", [D, T], F32, kind="ExternalOutput").ap()

    XS = dscr("XS", [D, NT])
    S_RX = dscr("S_RX", [D, NT])
    S_GRG = dscr("S_GRG", [D, NT], BF16)
    S_CQ = dscr("S_CQ", [384, NT])
    S_CKV = dscr("S_CKV", [256, NT])
    S_KR = dscr("S_KR", [128, NT])
    S_DQ = dscr("S_DQ", [8, 128, NT], BF16)
    S_DK = dscr("S_DK", [8, 128, NT], BF16)
    S_MG = dscr("S_MG", [3 * D, NT], BF16)
    S_DV = dscr("S_DV", [NT, D], BF16)
    if debug:
        S_HT = dscr("S_HT", [D, NT], BF16)
        S_UG = dscr("S_UG", [D, NT], BF16)
        S_OB = dscr("S_OB", [D, NT], BF16)
        S_OC = dscr("S_OC", [D, NT], BF16)
        S_MRG = dscr("S_MRG", [D, NT], BF16)
        S_X1 = dscr("S_X1", [D, NT])
        S_MOD = dscr("S_MOD", [128, 48, 2])

    hT = sb("hT", [128, KC, NT], BF16)
    mrg = sb("mrg", [128, KC, NT], BF16)
    ident = sb("identb", [128, 128], BF16)
    masks = sb("masksb", [128, 3, 128], BF16)
    vecs = sb("vecs", [128, NV])
    modT = sb("modT", [128, 48, 2])
    cvec = sb("cvec", [128, KC, 2])
    scb = sb("scb", [128, KC, 2], BF16)
    coef = sb("coef", [128, 6, KC, 2])
    lams = sb("lams", [128, 16])
    lams2 = sb("lams2", [128, 16])
    lamw = sb("lamw", [128, 256])
    lamv = sb("lamv", [128, 4])
    subg = sb("subg", [128, 128])
    NWB = 3
    wbuf = [sb(f"wbuf{i}", [128, KC, 512], BF16) for i in range(NWB)]
    wctr = [0]
    WORK_BYTES = 100 * 1024
    work = sb("work", [128, WORK_BYTES // 4])

    class Carver:
        def __init__(self):
            self.off = 0

        def take(self, shape, dt=F32):
            n = 1
            for s in shape[1:]:
                n *= s
            nbytes = n * (2 if dt == BF16 else 4)
            nwords = (nbytes + 3) // 4
            assert self.off + nwords <= WORK_BYTES // 4, (self.off, nwords)
            v = work[:, self.off:self.off + nwords]
            self.off += nwords
            if dt == BF16:
                v = v.bitcast(BF16)[:, 0:n]
            if len(shape) == 3:
                v = v.rearrange("p (a b) -> p a b", a=shape[1])
            elif len(shape) == 4:
                v = v.rearrange("p (a b c) -> p a b c", a=shape[1], b=shape[2])
            if shape[0] < 128:
                v = v[0:shape[0]]
            return v

    pall = st.enter_context(nc.psum_tensor("pall", [128, 8 * 512], F32))
    pbank = [pall[:, i * 512:(i + 1) * 512] for i in range(8)]

    P.dma("pool", ident[:], ident_in)
    P.dma("pool", masks[:], masks_in)
    P.dma("sp", cvec[:], cvec_in)
    ACT(scb[:], cvec[:], AF.Silu)
    P.dma("sp", XS, xT_in)

    ones_l = masks[:, 0, :]
    m96 = masks[:, 1, 0:96]
    b64 = masks[:, 2, :]

    def load_w(dst, src):
        return P.dma("pool", dst, src)

    def next_wbuf():
        b = wbuf[wctr[0] % NWB]
        wctr[0] += 1
        return b

    def gemm_fm(wsrc, kc_n, act, units, tblocks, evac, banks):
        bi = 0
        i = 0
        while i < len(units):
            lo = min(c[0] for c in units[i])
            j = i
            hi = lo
            while j < len(units):
                uhi = max(c[0] + c[1] for c in units[j])
                if uhi - lo > 512:
                    break
                hi = uhi
                j += 1
            assert j > i
            wt = next_wbuf()
            ncol = hi - lo
            load_w(wt[:, 0:kc_n, 0:ncol], wsrc[:, lo:hi].rearrange("(k p) n -> p k n", p=128))
            for u in units[i:j]:
                for (t0, n) in tblocks:
                    pss = []
                    for ch in u:
                        c0, m = ch[0], ch[1]
                        prow = ch[2] if len(ch) > 2 else 0
                        pb = banks[bi % len(banks)]
                        bi += 1
                        for k in range(kc_n):
                            MM(pb[prow:prow + m, 0:n], wt[:, k, c0 - lo:c0 - lo + m], act[:, k, t0:t0 + n], k == 0, k == kc_n - 1)
                        pss.append(pb)
                    evac(u, t0, n, pss)
            i = j

    fin_ops = []
    for l in range(n_layers):
        w = W[l]
        last = (l == n_layers - 1) and not force_ctx
        lam_init = 0.8 - 0.6 * float(np.exp(-0.3 * l))
        P.dma("sp", vecs[:], w["vecs"])
        P.dma("sp", lamw[:], w["dlam"].broadcast_to([128, 256]))
        P.dma("sp", subg[:], w["subg"].broadcast_to([128, 128]))
        TS("dve", subg[:], subg[:], 1.0 - lam_init, ALU.mult)
        cv = Carver()
        ltmp = cv.take([128, 128])
        lsum = cv.take([128, 2])
        TT("dve", ltmp[:, 0:64], lamw[:, 0:64], lamw[:, 64:128], ALU.mult)
        TT("dve", ltmp[:, 64:128], lamw[:, 128:192], lamw[:, 192:256], ALU.mult)
        P.op("dve", lambda e: e.reduce_sum(out=lsum[:, 0:1], in_=ltmp[:, 0:64], axis=mybir.AxisListType.X),
             reads=[ltmp[:, 0:64]], writes=[lsum[:, 0:1]])
        P.op("dve", lambda e: e.reduce_sum(out=lsum[:, 1:2], in_=ltmp[:, 64:128], axis=mybir.AxisListType.X),
             reads=[ltmp[:, 64:128]], writes=[lsum[:, 1:2]])
        ACT(lsum[:], lsum[:], AF.Exp)
        TT("dve", lamv[:, 1:2], lsum[:, 1:2], lsum[:, 0:1], ALU.subtract)
        TS("dve", lamv[:, 0:1], lamv[:, 1:2], -lam_init, ALU.add)
        ACT(lams2[:], vecs[:, V_LAM:V_LAM + 16], AF.Exp, scale=-1.0)
        ACT(lams2[:], lams2[:], AF.Ln, bias=1.0)
        TS("dve", lams[:], lams2[:], -8.0, ALU.mult)
        TS("dve", lams2[:], lams2[:], -16.0, ALU.mult)

        def evac_mod(u, t0, n, pss):
            ch = u[0][0] // 128
            TS("dve", modT[:, ch, :], pss[0][:, 0:2], vecs[:, V_BMOD + ch:V_BMOD + ch + 1], ALU.add)
        gemm_fm(w["wmod"], KC, scb, [[(c * 128, 128)] for c in range(48)], [(0, 2)], evac_mod, pbank[0:2])
        if debug and l == 0:
            P.dma("sp", S_MOD, modT[:])
        for j in range(2):
            for (ia, ib, ig, base, gv) in ((0, 1, 2, 0, V_N1G), (3, 4, 5, 24, V_N2G)):
                TS("dve", coef[:, ia, :, j], modT[:, base + 8:base + 16, j], 1.0, ALU.add, 32.0, ALU.mult)
                TT("dve", coef[:, ia, :, j], coef[:, ia, :, j], vecs[:, gv:gv + 8], ALU.mult)
                CP("dve", coef[:, ib, :, j], modT[:, base:base + 8, j])
                CP("dve", coef[:, ig, :, j], modT[:, base + 16:base + 24, j])

        def norm_block(xb, t0, n, ia, ib, cvn, bank):
            j = 0 if t0 < T else 1
            sq = cvn["sq"]
            for k in range(KC):
                ACT(sq[:, k, 0:n], xb[:, k, 0:n], AF.Square)
            for k in range(KC):
                MM(bank[:, 0:n], ones_l, sq[:, k, 0:n], k == 0, k == KC - 1)
            rs = cvn["rs"]
            RSTD(rs[:, 0:n], bank[:, 0:n], 1.0, float(D * EPS))
            tmp = cvn["tmp"]
            for k in range(KC):
                STT("dve" if k % 2 == 0 else "pool", tmp[:, k, 0:n], xb[:, k, 0:n], coef[:, ia, k, j:j + 1], rs[:, 0:n], ALU.mult, ALU.mult)
                ACT(hT[:, k, t0:t0 + n], tmp[:, k, 0:n], AF.Identity, bias=coef[:, ib, k, j:j + 1])

        cv = Carver()
        cvn = dict(sq=cv.take([128, KC, 512], BF16), rs=cv.take([128, 512]), tmp=cv.take([128, KC, 512]))
        xbs = [cv.take([128, KC, 512]) for _ in range(2)]
        for bi_, (t0, n) in enumerate(TBLK):
            xb = xbs[bi_ % 2]
            P.dma("sp", xb[:, :, 0:n], XS[:, t0:t0 + n].rearrange("(k p) t -> p k t", p=128))
            norm_block(xb, t0, n, 0, 1, cvn, pbank[6])
        if debug and l == 0:
            P.dma("sp", S_HT.rearrange("(k p) t -> p k t", p=128), hT[:])

        if stop_after == "norm":
            break
        cv = Carver()
        ropeD = cv.take([128, 2, NT])
        P.dma("sp", ropeD, ropeD_in.rearrange("a p t -> p a t"))
        stg = [cv.take([128, NT]) for _ in range(2)]
        stgb = [cv.take([128, NT], BF16) for _ in range(2)]
        sctr = [0]
        wk = dict(sq=cv.take([128, 512], BF16), rs=cv.take([128, 512]), t1=cv.take([128, 512]), t2=cv.take([128, 512]))

        def stage(bf):
            s = (stgb if bf else stg)[sctr[0] % 2]
            return s

        def evac_win(u, t0, n, pss):
            c0 = u[0][0]
            lastb = (t0 + n == NT)
            if c0 < O_RG:
                s = stg[(c0 // 128) % 2]
                CP("act", s[:, t0:t0 + n], pss[0][:, 0:n])
                if lastb:
                    P.dma("sp", S_RX[c0:c0 + 128, :], s[:])
            elif c0 < O_CQ:
                r0 = c0 - O_RG
                s = stgb[(c0 // 128) % 2]
                ACT(wk["t1"][:, 0:n], pss[0][:, 0:n], AF.Square)
                TS("dve", wk["t1"][:, 0:n], wk["t1"][:, 0:n], 0.044715, ALU.mult, 1.0, ALU.add)
                TT("dve", wk["t1"][:, 0:n], wk["t1"][:, 0:n], pss[0][:, 0:n], ALU.mult)
                ACT(wk["t1"][:, 0:n], wk["t1"][:, 0:n], AF.Sigmoid, scale=1.5957691216057308)
                TT("dve", s[:, t0:t0 + n], wk["t1"][:, 0:n], pss[0][:, 0:n], ALU.mult)
                if lastb:
                    P.dma("sp", S_GRG[r0:r0 + 128, :], s[:])
            elif c0 < O_KR:
                s = stg[(c0 // 128) % 2]
                CP("act", s[:, t0:t0 + n], pss[0][:, 0:n])
                if lastb:
                    if c0 < O_CKV:
                        P.dma("sp", S_CQ[c0 - O_CQ:c0 - O_CQ + 128, :], s[:])
                    else:
                        P.dma("sp", S_CKV[c0 - O_CKV:c0 - O_CKV + 128, :], s[:])
            elif c0 < O_DQ:
                s = stg[(c0 // 128) % 2]
                CP("act", s[64:128, t0:t0 + n], pss[0][64:128, 0:n])
                if lastb:
                    P.dma("sp", S_KR[64:128, :], s[64:128, :])
            elif c0 < O_MG:
                isk = c0 >= O_DK
                hh = (c0 - (O_DK if isk else O_DQ)) // 256
                vg = V_GDK if isk else V_GDQ
                s = stgb[hh % 2]
                po, psw = pss
                ACT(wk["sq"][:, 0:n], po[:, 0:n], AF.Square)
                MM(pbank[6][:, 0:n], b64, wk["sq"][:, 0:n], True, True)
                RSTD(wk["rs"][:, 0:n], pbank[6][:, 0:n], 1.0 / 64, EPS)
                STT("dve", wk["t1"][:, 0:n], po[:, 0:n], vecs[:, vg:vg + 1], ropeD[:, 0, t0:t0 + n], ALU.mult, ALU.mult)
                STT("dve", wk["t2"][:, 0:n], psw[:, 0:n], vecs[:, vg + 1:vg + 2], ropeD[:, 1, t0:t0 + n], ALU.mult, ALU.mult)
                TT("pool", wk["t1"][:, 0:n], wk["t1"][:, 0:n], wk["t2"][:, 0:n], ALU.add)
                TT("pool", s[:, t0:t0 + n], wk["t1"][:, 0:n], wk["rs"][:, 0:n], ALU.mult)
                if lastb:
                    P.dma("sp", (S_DK if isk else S_DQ)[hh], s[:])
            else:
                r0 = c0 - O_MG
                s = stgb[(c0 // 128) % 2]
                ACT(s[:, t0:t0 + n], pss[0][:, 0:n], AF.Sigmoid)
                if lastb:
                    P.dma("sp", S_MG[r0:r0 + 128, :], s[:])

        units = [[(c, 128)] for c in range(O_RX, O_KR, 128)]
        units.append([(O_KR, 64, 64)])
        gemm_fm(w["win"], KC, hT, units, TBLK, evac_win, pbank[0:4])
        if stop_after == "win1":
            break
        units = [[(c, 128), (c + 128, 128)] for c in range(O_DQ, O_MG, 256)]
        gemm_fm(w["win"], KC, hT, units, TBLK, evac_win, pbank[0:4])
        if stop_after == "win2":
            break
        units = [[(c, 128)] for c in range(O_MG, O_DV, 128)]
        gemm_fm(w["win"], KC, hT, units, TBLK, evac_win, pbank[0:4])
        if stop_after == "win3":
            break
        wdv = [next_wbuf(), next_wbuf()]
        for i_ in range(2):
            load_w(wdv[i_][:], w["win"][:, O_DV + i_ * 512:O_DV + (i_ + 1) * 512].rearrange("(k p) n -> p k n", p=128))
        for tt in range(NT // 128):
            s = stgb[tt % 2]
            for i_ in range(2):
                pb = pbank[(tt * 2 + i_) % 4]
                for k in range(KC):
                    MM(pb[:, :], hT[:, k, tt * 128:(tt + 1) * 128], wdv[i_][:, k, :], k == 0, k == KC - 1)
                CP("act" if i_ == 0 else "dve", s[:, i_ * 512:(i_ + 1) * 512], pb[:, :])
            P.dma("sp", S_DV[tt * 128:(tt + 1) * 128, :], s[:, 0:1024])

        if stop_after == "win":
            break
        cv = Carver()
        brT = hT
        LP = 2051
        xpad = cv.take([128, 2310])
        xc = cv.take([128, NT])
        xcb = cv.take([128, NT], BF16)
        g_r = cv.take([128, NT])
        g_i = cv.take([128, NT])
        av = g_r
        bv = cv.take([128, NT])
        lruw = cv.take([128, 32, 128], BF16)
        load_w(lruw, w["lru"])
        hf = cv.take([128, NT])
        hb = cv.take([128, NT])
        grg = cv.take([128, NT], BF16)
        MSET("pool", xpad[:, :], 0.0)
        for j in range(KC):
            P.dma("sp", xpad[:, 2:2 + T], S_RX[j * 128:(j + 1) * 128, 0:T])
            P.dma("sp", xpad[:, LP + 2:LP + 2 + C], S_RX[j * 128:(j + 1) * 128, T:NT])
            P.dma("sp", grg[:], S_GRG[j * 128:(j + 1) * 128, :])
            cw = lambda tap: vecs[:, V_CONVW + tap * 8 + j:V_CONVW + tap * 8 + j + 1]
            for (o0, base, n) in ((0, 0, T), (T, LP, C)):
                ACT(xc[:, o0:o0 + n], xpad[:, base:base + n], AF.Identity, bias=vecs[:, V_CONVB + j:V_CONVB + j + 1], scale=cw(0))
                for tap in (1, 2, 3):
                    STT("dve", xc[:, o0:o0 + n], xpad[:, base + tap:base + tap + n], cw(tap), xc[:, o0:o0 + n], ALU.mult, ALU.add)
            CP("pool", xcb[:], xc[:])
            for r in range(2):
                for gi, (gdst, vb_) in enumerate(((g_r, V_BA), (g_i, V_BI))):
                    wsl = lruw[:, gi * 16 + r * 8 + j, :]
                    for bi_, (t0, n) in enumerate(TBLK):
                        pb = pbank[bi_ % 4]
                        MM(pb[:, 0:n], wsl, xcb[:, t0:t0 + n], True, True)
                        ACT(gdst[:, t0:t0 + n], pb[:, 0:n], AF.Sigmoid, bias=vecs[:, vb_ + r * 8 + j:vb_ + r * 8 + j + 1])
                ACT(bv[:], g_r[:], AF.Exp, scale=lams2[:, r * 8 + j:r * 8 + j + 1])
                ACT(av[:], g_r[:], AF.Exp, scale=lams[:, r * 8 + j:r * 8 + j + 1])
                ACT(bv[:], bv[:], AF.Sqrt, scale=-1.0, bias=1.0)
                TT("pool", bv[:], bv[:], g_i[:], ALU.mult)
                TT("pool", bv[:], bv[:], xc[:], ALU.mult)
                if r == 0:
                    SCAN(hf[:, T:NT], av[:, T:NT], bv[:, T:NT], 0.0)
                    SCAN(hf[:, 0:T], av[:, 0:T], bv[:, 0:T], hf[:, NT - 1:NT])
                else:
                    SCAN(hb[:, T:NT][:, ::-1], av[:, T:NT][:, ::-1], bv[:, T:NT][:, ::-1], 0.0)
                    SCAN(hb[:, 0:T][:, ::-1], av[:, 0:T][:, ::-1], bv[:, 0:T][:, ::-1], hb[:, T:T + 1])
            TT("pool", hf[:], hf[:], hb[:], ALU.add)
            TT("pool", brT[:, j, :], hf[:], grg[:], ALU.mult)
        if debug and l == 0:
            P.dma("sp", S_UG.rearrange("(k p) t -> p k t", p=128), brT[:])

        def branch_merge(wsrc, bidx, cvb):
            gts = cvb["gts"]
            tmpb = cvb["tmpb"]

            def evac_br(u, t0, n, pss):
                o = u[0][0] // 128
                g = gts[0]
                if t0 == 0:
                    P.dma("sp", g[:], S_MG[bidx * D + o * 128:bidx * D + (o + 1) * 128, :])
                if bidx == 0:
                    TT("dve", mrg[:, o, t0:t0 + n], pss[0][:, 0:n], g[:, t0:t0 + n], ALU.mult)
                else:
                    TT("dve", tmpb[:, 0:n], pss[0][:, 0:n], g[:, t0:t0 + n], ALU.mult)
                    TT("pool", mrg[:, o, t0:t0 + n], mrg[:, o, t0:t0 + n], tmpb[:, 0:n], ALU.add)
            gemm_fm(wsrc, KC, brT, [[(c * 128, 128)] for c in range(KC)], LBLK, evac_br, pbank[0:4])

        LBLK = TBLK[0:4] if last else TBLK
        cvb = dict(gts=[cv.take([128, NT], BF16)], tmpb=cv.take([128, 512]))
        branch_merge(w["wbra"], 0, cvb)

        if stop_after == "A":
            break
        cv = Carver()
        cvb = dict(gts=[cv.take([128, NT], BF16)], tmpb=cv.take([128, 512]))
        ropeM = cv.take([128, NT])
        P.dma("sp", ropeM, ropeM_in[0])
        cqn = cv.take([128, 3, NT], BF16)
        ckvn = cv.take([128, 2, NT], BF16)
        krr = cv.take([128, NT])
        sqk = cv.take([128, NT], BF16)
        QT = [cv.take([128, NT], BF16) for _ in range(2)]
        KT = [cv.take([128, NT], BF16) for _ in range(2)]
        Vh = [cv.take([128, 18, 65], BF16) for _ in range(2)]
        Otm = cv.take([128, 18, 128], BF16)
        PT2 = [cv.take([128, 2, 512], BF16) for _ in range(3)]
        PT = [PT2[0][:, 0, :], PT2[0][:, 1, :], PT2[1][:, 0, :]]
        ld_ = cv.take([128, 3, 512])
        wk = dict(sq=cv.take([128, 512], BF16), rs=cv.take([128, 512]), rs2=ld_[:, 0, :], t1=ld_[:, 1, :], t2=ld_[:, 2, :],
                  ld=ld_, rc=cv.take([128, 4]))
        ptrM = pbank[6].bitcast(BF16)
        MSET("pool", sqk[:, :], 0.0)
        for v_ in Vh:
            MSET("pool", v_[:, :, 64:65], 1.0)
        for (src, nch, dst, gv) in ((S_CQ, 3, cqn, V_QNG), (S_CKV, 2, ckvn, V_KVNG)):
            for (t0, n) in TBLK:
                ld = wk["ld"]
                P.dma("sp", ld[:, 0:nch, 0:n], src[:, t0:t0 + n].rearrange("(k p) t -> p k t", p=128))
                for k in range(nch):
                    ACT(PT[k][:, 0:n], ld[:, k, 0:n], AF.Square)
                for k in range(nch):
                    MM(pbank[6][:, 0:n], ones_l, PT[k][:, 0:n], k == 0, k == nch - 1)
                RSTD(wk["rs"][:, 0:n], pbank[6][:, 0:n], 1.0 / (nch * 128), EPS)
                for k in range(nch):
                    STT("dve", dst[:, k, t0:t0 + n], ld[:, k, 0:n], vecs[:, gv + k:gv + k + 1], wk["rs"][:, 0:n], ALU.mult, ALU.mult)
        for (t0, n) in TBLK:
            ld = wk["ld"]
            P.dma("sp", ld[64:128, 0, 0:n], S_KR[64:128, t0:t0 + n])
            ACT(sqk[64:96, t0:t0 + n], ld[64:96, 0, 0:n], AF.Square)
            STT("dve", krr[64:96, t0:t0 + n], ld[64:96, 0, 0:n], vecs[64:96, V_GK:V_GK + 1], ropeM[64:96, t0:t0 + n], ALU.mult, ALU.mult)
            CP("act", wk["t2"][64:96, 0:n], ld[96:128, 0, 0:n])
            P.dma("sp", wk["t1"][64:96, 0:n], ropeM_in[1, 64:96, t0:t0 + n])
            TT("dve", wk["t2"][64:96, 0:n], wk["t2"][64:96, 0:n], wk["t1"][64:96, 0:n], ALU.mult)
            STT("dve", krr[64:96, t0:t0 + n], wk["t2"][64:96, 0:n], vecs[64:96, V_GKSW:V_GKSW + 1], krr[64:96, t0:t0 + n], ALU.mult, ALU.add)
        wq = next_wbuf()
        wq2 = next_wbuf()
        wkk = next_wbuf()
        wqv = [b_.rearrange("p k n -> p (k n)")[:, 0:3072].rearrange("p (k n) -> p k n", k=3) for b_ in (wq, wq2)]
        for i_ in range(2):
            load_w(wqv[i_], w["wuq"][:, i_ * 1024:(i_ + 1) * 1024].rearrange("(k p) n -> p k n", p=128))
        wkv = wkk.rearrange("p k n -> p (k n)").rearrange("p (a k n) -> p a k n", a=2, k=2)
        load_w(wkv[:, 0], w["wukvk"].rearrange("(k p) n -> p k n", p=128))
        load_w(wkv[:, 1], w["wukvv"].rearrange("(k p) n -> p k n", p=128))

        QBLK = [(0, 512, 0, 18), (512, 512, 0, 18), (1024, 512, 0, 18), (1536, 512, 0, 18), (2048, 256, 16, 18)]
        if last:
            QBLK = QBLK[0:4]
        def mla_prep_steps(hd):
            Qh, Kh, V_ = QT[hd % 2], KT[hd % 2], Vh[hd % 2]
            wqh = wqv[hd // 8][:, :, (hd % 8) * 128:(hd % 8 + 1) * 128]
            steps = []
            for bi_, (t0, n) in enumerate(TBLK):
                def qstep(t0=t0, n=n):
                    pq = pbank[6]
                    for k in range(3):
                        MM(pq[:, 0:n], wqh[:, k, :], cqn[:, k, t0:t0 + n], k == 0, k == 2)
                    ACT(wk["sq"][:, 0:n], pq[:, 0:n], AF.Square)
                    MM(pbank[7][0:96, 0:n], m96, wk["sq"][:, 0:n], True, True)
                    RSTD(wk["rs"][0:96, 0:n], pbank[7][0:96, 0:n], 1.0 / 96, EPS)
                    TS("dve", wk["t1"][0:64, 0:n], pq[0:64, 0:n], vecs[0:64, V_GQ:V_GQ + 1], ALU.mult)
                    STT("dve", wk["t1"][64:96, 0:n], pq[64:96, 0:n], vecs[64:96, V_GQ:V_GQ + 1], ropeM[64:96, t0:t0 + n], ALU.mult, ALU.mult)
                    TT("dve", wk["t2"][64:96, 0:n], pq[96:128, 0:n], ropeM[0:32, t0:t0 + n], ALU.mult)
                    STT("dve", wk["t1"][64:96, 0:n], wk["t2"][64:96, 0:n], vecs[64:96, V_GQSW:V_GQSW + 1], wk["t1"][64:96, 0:n], ALU.mult, ALU.add)
                    TT("pool", Qh[0:96, t0:t0 + n], wk["t1"][0:96, 0:n], wk["rs"][0:96, 0:n], ALU.mult)
                steps.append(qstep)

                def kstep(t0=t0, n=n):
                    pk = pbank[6]
                    for k in range(2):
                        MM(pk[0:64, 0:n], wkv[:, 0, k, hd * 64:(hd + 1) * 64], ckvn[:, k, t0:t0 + n], k == 0, k == 1)
                    ACT(sqk[0:64, t0:t0 + n], pk[0:64, 0:n], AF.Square)
                    MM(pbank[7][0:96, 0:n], m96, sqk[:, t0:t0 + n], True, True)
                    RSTD(wk["rs2"][0:96, 0:n], pbank[7][0:96, 0:n], 1.0 / 96, EPS)
                    STT("dve", Kh[0:64, t0:t0 + n], pk[0:64, 0:n], vecs[0:64, V_GK:V_GK + 1], wk["rs2"][0:64, 0:n], ALU.mult, ALU.mult)
                    TT("pool", Kh[64:96, t0:t0 + n], krr[64:96, t0:t0 + n], wk["rs2"][64:96, 0:n], ALU.mult)
                steps.append(kstep)
            for g4 in range(0, 18, 4):
                def vstep(g4=g4):
                    nt_ = min(4, 18 - g4)
                    pv = pbank[6]
                    for i_ in range(nt_):
                        tt = g4 + i_
                        for k in range(2):
                            MM(pv[:, i_ * 64:(i_ + 1) * 64], ckvn[:, k, tt * 128:(tt + 1) * 128], wkv[:, 1, k, hd * 64:(hd + 1) * 64], k == 0, k == 1)
                    CP("dve", V_[:, g4:g4 + nt_, 0:64], pv[:, 0:nt_ * 64].rearrange("p (a b) -> p a b", a=nt_))
                steps.append(vstep)
            return steps

        for s_ in mla_prep_steps(0):
            s_()
        gpair = 0
        for hd in range(16):
            Qh, Kh, V_ = QT[hd % 2], KT[hd % 2], Vh[hd % 2]
            nxt = mla_prep_steps(hd + 1) if hd + 1 < 16 else []
            pairs = [(qi, kt) for qi, (q0, qn, k0, k1) in enumerate(QBLK) for kt in range(k0, k1, 2)]
            n_p = len(pairs)

            def stA(j):
                qi, kt = pairs[j]
                q0, qn, k0, k1 = QBLK[qi]
                b0 = ((gpair + j) % 2) * 2
                for h2 in range(2):
                    MM(pbank[b0 + h2][:, 0:qn], Kh[0:96, (kt + h2) * 128:(kt + h2 + 1) * 128], Qh[0:96, q0:q0 + qn], True, True)
                src = pall[:, b0 * 512:(b0 + 2) * 512].rearrange("p (a b) -> p a b", a=2)[:, :, 0:qn]
                ACT(PT2[(gpair + j) % 3][:, :, 0:qn], src, AF.Exp, scale=MLA_SCALE)

            def stB(j):
                qi, kt = pairs[j]
                q0, qn, k0, k1 = QBLK[qi]
                nq = qn // 128
                po = pbank[4 + qi % 2]
                pt_ = PT2[(gpair + j) % 3]
                for h2 in range(2):
                    for qt in range(nq):
                        MM(po[:, qt * 65:(qt + 1) * 65], pt_[:, h2, qt * 128:(qt + 1) * 128], V_[:, kt + h2, :],
                           (kt + h2 == k0 and qt == 0), (kt + h2 == k1 - 1 and qt == nq - 1), skip_group_check=True)
                if kt + 2 == k1:
                    pov = po[:, 0:nq * 65].rearrange("p (a b) -> p a b", a=nq)
                    RECIP(wk["rc"][:, 0:nq], pov[:, :, 64])
                    tq0 = q0 // 128
                    TT("dve", Otm[:, tq0:tq0 + nq, (hd % 2) * 64:(hd % 2) * 64 + 64], pov[:, :, 0:64],
                       wk["rc"][:, 0:nq].unsqueeze(2).broadcast_to([128, nq, 64]), ALU.mult)

            for j in range(n_p + 1):
                if j >= 1:
                    stB(j - 1)
                if j < n_p:
                    stA(j)
                if j % 2 == 1 and nxt:
                    nxt.pop(0)()
            while nxt:
                nxt.pop(0)()
            gpair += n_p
            if hd % 2 == 1:
                for g4 in range(0, 18, 4):
                    nt_ = min(4, 18 - g4)
                    for i_ in range(nt_):
                        TR(ptrM[:, i_ * 128:(i_ + 1) * 128], Otm[:, g4 + i_, :], ident[:])
                    CP("dve", brT[:, hd // 2, g4 * 128:(g4 + nt_) * 128], ptrM[:, 0:nt_ * 128])
        if debug and l == 0:
            P.dma("sp", S_OB.rearrange("(k p) t -> p k t", p=128), brT[:])
        branch_merge(w["wbrb"], 1, cvb)

        if stop_after == "mla":
            break
        cv = Carver()
        cvb = dict(gts=[cv.take([128, NT], BF16)], tmpb=cv.take([128, 512]))
        QT = [cv.take([128, NT], BF16) for _ in range(2)]
        KT = [cv.take([128, NT], BF16) for _ in range(2)]
        Vd = [cv.take([128, 18, 129], BF16) for _ in range(2)]
        PT2 = [cv.take([128, 2, 512], BF16) for _ in range(3)]
        ptrD = pbank[0].bitcast(BF16)
        Otm = cv.take([128, 18, 128], BF16)
        wk = dict(rc=cv.take([128, 8]), o=cv.take([128, 4, 128]), t=cv.take([128, 4, 128]), ss=cv.take([128, 4]), junk=cv.take([128, 128]),
                  oraw=cv.take([128, 4, 258]))
        for v_ in Vd:
            MSET("pool", v_[:, :, 128:129], 1.0)
        gpair = 0
        for hd in range(8):
            Qh, Kh, V_ = QT[hd % 2], KT[hd % 2], Vd[hd % 2]
            P.dma("sp", Qh[:], S_DQ[hd])
            P.dma("sp", Kh[:], S_DK[hd])
            P.dma("sp", V_[:, :, 0:128], S_DV[:, hd * 128:(hd + 1) * 128].rearrange("(a p) d -> p a d", p=128))
            pairs = [(qi, kt) for qi, (q0, qn, k0, k1) in enumerate(QBLK) for kt in range(k0, k1)]
            n_p = len(pairs)

            def stA(j):
                qi, kt = pairs[j]
                q0, qn, k0, k1 = QBLK[qi]
                b0 = ((gpair + j) % 2) * 2
                for m in range(2):
                    r0 = m * 64
                    MM(pbank[b0 + m][:, 0:qn], Kh[r0:r0 + 64, kt * 128:(kt + 1) * 128], Qh[r0:r0 + 64, q0:q0 + qn], True, True)
                src = pall[:, b0 * 512:(b0 + 2) * 512].rearrange("p (a b) -> p a b", a=2)[:, :, 0:qn]
                ACT(PT2[(gpair + j) % 3][:, :, 0:qn], src, AF.Exp, scale=DIFF_SCALE)

            def stB(j):
                qi, kt = pairs[j]
                q0, qn, k0, k1 = QBLK[qi]
                nq = qn // 128
                pt_ = PT2[(gpair + j) % 3]
                for m in range(2):
                    for qt in range(nq):
                        ob, oo = pbank[4 + m * 2 + qt // 2], (qt % 2) * 129
                        MM(ob[:, oo:oo + 129], pt_[:, m, qt * 128:(qt + 1) * 128], V_[:, kt, :],
                           (kt == k0 and qt % 2 == 0), (kt == k1 - 1 and qt % 2 == 1), skip_group_check=True)
                if kt == k1 - 1:
                    oraw = wk["oraw"]
                    used = [0, 1, 2, 3] if nq == 4 else [0, 2]
                    for ci, b_ in enumerate(used):
                        CP("dve", oraw[:, b_, :], pbank[4 + b_][:, 0:258])
                    tq0 = q0 // 128
                    for qt in range(nq):
                        ob0, o0 = oraw[:, qt // 2, :], (qt % 2) * 129
                        ob1, o1 = oraw[:, 2 + qt // 2, :], (qt % 2) * 129
                        RECIP(wk["rc"][:, 0:1], ob0[:, o0 + 128:o0 + 129])
                        RECIP(wk["rc"][:, 1:2], ob1[:, o1 + 128:o1 + 129])
                        TT("dve", wk["rc"][:, 1:2], wk["rc"][:, 1:2], lamv[:, 0:1], ALU.mult)
                        TS("pool", wk["t"][:, qt, :], ob1[:, o1:o1 + 128], wk["rc"][:, 1:2], ALU.mult)
                        STT("dve", wk["o"][:, qt, :], ob0[:, o0:o0 + 128], wk["rc"][:, 0:1], wk["t"][:, qt, :], ALU.mult, ALU.add)
                        ACT(wk["junk"][:, :], wk["o"][:, qt, :], AF.Square, accum=wk["ss"][:, qt:qt + 1])
                    RSTD(wk["ss"][:, 0:nq], wk["ss"][:, 0:nq], 1.0 / 128, EPS)
                    for qt in range(nq):
                        STT("dve", Otm[:, tq0 + qt, :], wk["o"][:, qt, :], wk["ss"][:, qt:qt + 1], subg[:], ALU.mult, ALU.mult)

            for j in range(n_p + 1):
                if j >= 1:
                    stB(j - 1)
                if j < n_p:
                    stA(j)
            gpair += n_p
            for g4 in range(0, 18, 4):
                nt_ = min(4, 18 - g4)
                for i_ in range(nt_):
                    TR(ptrD[:, i_ * 128:(i_ + 1) * 128], Otm[:, g4 + i_, :], ident[:])
                CP("dve", brT[:, hd, g4 * 128:(g4 + nt_) * 128], ptrD[:, 0:nt_ * 128])
        if debug and l == 0:
            P.dma("sp", S_OC.rearrange("(k p) t -> p k t", p=128), brT[:])
        branch_merge(w["wbrc"], 2, cvb)
        if debug and l == 0:
            P.dma("sp", S_MRG.rearrange("(k p) t -> p k t", p=128), mrg[:])

        if stop_after == "diff":
            break
        cv = Carver()
        cvn = dict(sq=cv.take([128, KC, 512], BF16), rs=cv.take([128, 512]), tmp=cv.take([128, KC, 512]))
        xbs = [cv.take([128, KC, 512]) for _ in range(2)]
        wo = [next_wbuf(), next_wbuf()]
        for i_ in range(2):
            load_w(wo[i_][:], w["wout"][:, i_ * 512:(i_ + 1) * 512].rearrange("(k p) n -> p k n", p=128))
        oblks = TBLK if not last else TBLK[0:4]
        for bi_, (t0, n) in enumerate(oblks):
            xb = xbs[bi_ % 2]
            j = 0 if t0 < T else 1
            P.dma("sp", xb[:, :, 0:n], XS[:, t0:t0 + n].rearrange("(k p) t -> p k t", p=128))
            for o in range(KC):
                pb = pbank[o % 4]
                for k in range(KC):
                    MM(pb[:, 0:n], wo[o // 4][:, k, (o % 4) * 128:(o % 4 + 1) * 128], mrg[:, k, t0:t0 + n], k == 0, k == KC - 1)
                STT("dve", xb[:, o, 0:n], pb[:, 0:n], coef[:, 2, o, j:j + 1], xb[:, o, 0:n], ALU.mult, ALU.add)
            P.dma("sp", XS[:, t0:t0 + n].rearrange("(k p) t -> p k t", p=128), xb[:, :, 0:n])
            if debug and l == 0:
                P.dma("sp", S_X1[:, t0:t0 + n].rearrange("(k p) t -> p k t", p=128), xb[:, :, 0:n])
            norm_block(xb, t0, n, 3, 4, cvn, pbank[6])

        cv = Carver()
        if last:
            groups = [[(0, 512, 0), (512, 512, 512)], [(1024, 512, 0), (1536, 512, 512)]]
        else:
            groups = [[(0, 512, 0), (512, 512, 512), (2048, 256, 1024)], [(1024, 512, 0), (1536, 512, 512)]]
        half = 1280 if not last else 1024
        actT = cv.take([128, 22, half], BF16)
        gbuf = [cv.take([128, 512]) for _ in range(2)]
        xio = [cv.take([128, 512]) for _ in range(3)]
        xctr = [0]
        wfo = cv.take([128, 22, 512], BF16)
        for grp in range(2):
            blks = [(a_, b_) for (a_, b_, _) in groups[grp]]
            aoff = {a_: c_ for (a_, _, c_) in groups[grp]}

            def evac_ffi(u, t0, n, pss, aoff=aoff):
                f = u[0][0] // 256
                gb = gbuf[f % 2]
                ACT(gb[:, 0:n], pss[0][:, 0:n], AF.Silu)
                TT("dve", actT[:, f, aoff[t0]:aoff[t0] + n], pss[1][:, 0:n], gb[:, 0:n], ALU.mult)
            gemm_fm(w["wffi"], KC, hT, [[(f * 256, 128), (f * 256 + 128, 128)] for f in range(22)], blks, evac_ffi, pbank[0:4])
            for oc in range(2):
                load_w(wfo[:], w["wffo"][:, oc * 512:(oc + 1) * 512].rearrange("(k p) n -> p k n", p=128))
                for o4 in range(4):
                    o = oc * 4 + o4
                    for (t0, n) in blks:
                        j = 0 if t0 < T else 1
                        xb = xio[xctr[0] % 3]
                        xctr[0] += 1
                        P.dma("sp", xb[:, 0:n], XS[o * 128:(o + 1) * 128, t0:t0 + n])
                        pb = pbank[4 + (xctr[0] % 2)]
                        for k in range(22):
                            MM(pb[:, 0:n], wfo[:, k, o4 * 128:(o4 + 1) * 128], actT[:, k, aoff[t0]:aoff[t0] + n], k == 0, k == 21)
                        STT("dve", xb[:, 0:n], pb[:, 0:n], coef[:, 5, o, j:j + 1], xb[:, 0:n], ALU.mult, ALU.add)
                        if last:
                            fin_ops.append(P.dma("sp", outT[o * 128:(o + 1) * 128, t0:t0 + n], xb[:, 0:n]))
                        else:
                            P.dma("sp", XS[o * 128:(o + 1) * 128, t0:t0 + n], xb[:, 0:n])

    P.emit(final_wait_ops=fin_ops + (list(P.dma_last.values()) if debug else []))
    st.close()
    return nc, P


def _swap_pairs(a, axis=-1):
    a = np.moveaxis(a, axis, -1)
    sh = a.shape
    b = a.reshape(sh[:-1] + (sh[-1] // 2, 2))[..., ::-1].reshape(sh)
    return np.moveaxis(b, -1, axis)


def _rope_tables(rot_dim):
    rows = T // 64
    row_ids = np.repeat(np.arange(rows, dtype=np.float32), 64)
    col_ids = np.tile(np.arange(64, dtype=np.float32), rows)
    n = rot_dim // 4
    freqs = (np.float32(10000.0) ** (-np.arange(n, dtype=np.float32) / np.float32(n))).astype(np.float32)
    ang = np.concatenate([row_ids[:, None] * freqs, col_ids[:, None] * freqs], axis=-1).astype(np.float32)
    cos = np.cos(ang).astype(np.float32)
    sin = np.sin(ang).astype(np.float32)
    ctab = np.ones((rot_dim, NT), np.float32)
    stab = np.zeros((rot_dim, NT), np.float32)
    ctab[:, 0:T] = np.repeat(cos.T, 2, axis=0)
    s2 = np.repeat(sin.T, 2, axis=0)
    s2[0::2] *= -1.0
    stab[:, 0:T] = s2
    return ctab, stab


_CONST = {}


def _consts():
    if _CONST:
        return _CONST
    cm, sm = _rope_tables(32)
    ropeM = np.zeros((2, 128, NT), np.float32)
    ropeM[0, 64:96] = cm
    ropeM[0, 0:32] = sm
    ropeM[1, 64:96] = sm
    cd, sd = _rope_tables(64)
    ropeD = np.zeros((2, 128, NT), np.float32)
    ropeD[0] = np.concatenate([cd, cd], axis=0)
    ropeD[1] = np.concatenate([sd, sd], axis=0)
    masks = np.zeros((128, 3, 128), np.float32)
    masks[:, 0, :] = 1.0
    masks[0:96, 1, 0:96] = 1.0
    masks[0:64, 2, 0:64] = 1.0
    masks[64:128, 2, 64:128] = 1.0
    _CONST.update(ident=np.eye(128, dtype=np.float32), masks=masks, ropeM=ropeM, ropeD=ropeD)
    return _CONST


def _pcol(v):
    return np.ascontiguousarray(v.reshape(-1, 128).T)


def _prep_layer(inp, l):
    f = lambda k: np.asarray(inp[k][l], dtype=np.float32)
    w_in = f("w_in")
    ext = np.empty((D, NEXT), np.float32)
    ext[:, O_RX:O_RX + 1024] = w_in[:, 0:1024]
    ext[:, O_RG:O_RG + 1024] = w_in[:, 1024:2048]
    ext[:, O_CQ:O_CQ + 384] = w_in[:, 2048:2432]
    ext[:, O_CKV:O_CKV + 256] = w_in[:, 2432:2688]
    kr = w_in[:, 2688:2720]
    ext[:, O_KR:O_KR + 32] = kr
    ext[:, O_KR + 32:O_KR + 64] = _swap_pairs(kr)
    for (src, dst) in ((2720, O_DQ), (3744, O_DK)):
        for h in range(8):
            blk = w_in[:, src + h * 128:src + (h + 1) * 128]
            ext[:, dst + h * 256:dst + h * 256 + 128] = blk
            ext[:, dst + h * 256 + 128:dst + (h + 1) * 256] = _swap_pairs(blk.reshape(D, 2, 64)).reshape(D, 128)
    ext[:, O_MG:O_MG + 3072] = w_in[:, 5792:8864]
    ext[:, O_DV:O_DV + 1024] = w_in[:, 4768:5792]
    uq = f("mla_w_uq").reshape(384, 16, 96)
    wuq = np.concatenate([uq, _swap_pairs(uq[:, :, 64:96])], axis=2).reshape(384, 2048)
    ukv = f("mla_w_ukv").reshape(256, 16, 128)
    wukvk = np.ascontiguousarray(ukv[:, :, 0:64]).reshape(256, 1024)
    wukvv = np.ascontiguousarray(ukv[:, :, 64:128]).reshape(256, 1024)
    lru = np.stack([f("lru_wa"), f("lru_wi")], axis=0)
    lru = np.ascontiguousarray(lru.transpose(3, 0, 1, 2, 4)).reshape(128, 32, 128)
    ffi = f("w_ffn_in").reshape(D, 2, 22, 128).transpose(0, 2, 1, 3).reshape(D, 2 * FFH)
    vecs = np.zeros((128, NV), np.float32)
    vecs[:, V_BMOD:V_BMOD + 48] = _pcol(f("b_mod"))
    vecs[:, V_N1G:V_N1G + 8] = _pcol(f("norm1_g"))
    vecs[:, V_N2G:V_N2G + 8] = _pcol(f("norm2_g"))
    cw = f("conv_w")
    for tap in range(4):
        vecs[:, V_CONVW + tap * 8:V_CONVW + tap * 8 + 8] = _pcol(cw[tap])
    vecs[:, V_CONVB:V_CONVB + 8] = _pcol(f("conv_b"))
    for (nm, vb) in (("lru_ba", V_BA), ("lru_bi", V_BI), ("lru_lambda", V_LAM)):
        a = f(nm)
        for r in range(2):
            vecs[:, vb + r * 8:vb + r * 8 + 8] = _pcol(a[r])
    vecs[:, V_QNG:V_QNG + 3] = _pcol(f("mla_qn_g"))
    vecs[:, V_KVNG:V_KVNG + 2] = _pcol(f("mla_kvn_g"))
    for (nm, vb) in (("mla_q_g", V_GQ), ("mla_k_g", V_GK)):
        g = f(nm)
        vecs[0:96, vb] = g
        vecs[64:96, vb + 1] = _swap_pairs(g[64:96])
    for (nm, vb) in (("diff_q_g", V_GDQ), ("diff_k_g", V_GDK)):
        g = f(nm)
        vecs[:, vb] = np.concatenate([g, g])
        vecs[:, vb + 1] = np.concatenate([_swap_pairs(g), _swap_pairs(g)])
    return {
        f"wmod{l}": f("w_mod"), f"win{l}": ext, f"wuq{l}": np.ascontiguousarray(wuq), f"wukvk{l}": wukvk, f"wukvv{l}": wukvv,
        f"lru{l}": lru, f"wbra{l}": f("w_br_a"), f"wbrb{l}": f("w_br_b"), f"wbrc{l}": f("w_br_c"), f"wout{l}": f("w_out"),
        f"wffi{l}": np.ascontiguousarray(ffi), f"wffo{l}": f("w_ffn_out"), f"vecs{l}": vecs,
        f"dlam{l}": np.ascontiguousarray(f("diff_lambda").reshape(1, 256)), f"subg{l}": np.ascontiguousarray(f("diff_subln_g").reshape(1, 128)),
    }


def _prep_core(inp, b):
    x = np.asarray(inp["x"][b], dtype=np.float32)
    ctx = np.asarray(inp["ctx"][b], dtype=np.float32)
    xT = np.ascontiguousarray(np.concatenate([x.T, ctx.T], axis=1))
    cv = np.stack([_pcol(np.asarray(inp["c"][b], np.float32)), _pcol(np.asarray(inp["c_ctx"], np.float32))], axis=2)
    return {"xT": xT, "cvec": np.ascontiguousarray(cv)}


_PROG = {}


def kernel(**inputs):
    n_cores = 8
    if "nc" not in _PROG:
        _PROG["nc"] = build(DEPTH, debug=False)[0]
    nc = _PROG["nc"]
    shared = dict(_consts())
    for l in range(DEPTH):
        shared.update(_prep_layer(inputs, l))
    in_maps = []
    for b in range(n_cores):
        m = dict(shared)
        m.update(_prep_core(inputs, b))
        in_maps.append(m)
    res = run_bass_kernel_spmd(nc, in_maps, core_ids=list(range(n_cores)))
    out = np.stack([np.asarray(r["outT"], dtype=np.float32).T for r in res.results], axis=0)
    return np.ascontiguousarray(out)
```

```python
import contextlib
import numpy as np
import concourse.bass as bass
import concourse.mybir as mybir
from concourse.bass_utils import run_bass_kernel_spmd

F32 = mybir.dt.float32
BF16 = mybir.dt.bfloat16
AF = mybir.ActivationFunctionType
ALU = mybir.AluOpType

D = 1024
T = 2048
C = 256
NT = T + C
KC = 8
DEPTH = 2
EPS = 1e-6
FFH = 2816
TBLK = [(0, 512), (512, 512), (1024, 512), (1536, 512), (2048, 256)]
MLA_FILL = 2
DIFF_FILL = 1
MLA_SCALE = 96 ** -0.5
DIFF_SCALE = 64 ** -0.5

O_RX, O_RG, O_CQ, O_CKV, O_KR, O_DQ, O_DK, O_MG, O_DV = 0, 1024, 2048, 2432, 2688, 2752, 4800, 6848, 9920
NEXT = 10944
V_BMOD, V_N1G, V_N2G, V_CONVW, V_CONVB, V_BA, V_BI, V_LAM, V_QNG, V_KVNG = 0, 48, 56, 64, 96, 104, 120, 136, 152, 155
V_GQ, V_GQSW, V_GK, V_GKSW, V_GDQ, V_GDQSW, V_GDK, V_GDKSW = 157, 158, 159, 160, 161, 162, 163, 164
NV = 165

SELF_SYNC = True


def _dsz(dt):
    return 2 if dt == BF16 else 4


def _box(ap):
    t = ap.tensor
    dims = ap.ap
    off = int(ap.offset)
    space = str(ap.space)
    z = _dsz(ap.dtype)
    if space in ("SB", "PSUM"):
        pstep, pcnt = dims[0]
        if pstep == 0:
            p0, f = 0, off
        else:
            p0 = off // pstep
            f = off - p0 * pstep
        lo = hi = f
        for st, cnt in dims[1:]:
            ext = st * (cnt - 1)
            if ext >= 0:
                hi += ext
            else:
                lo += ext
        return (t.name, p0, p0 + pcnt, lo * z, (hi + 1) * z)
    lo = hi = off
    for st, cnt in dims:
        ext = st * (cnt - 1)
        if ext >= 0:
            hi += ext
        else:
            lo += ext
    return (t.name, 0, 1, lo * z, (hi + 1) * z)


def _overlap(a, b):
    return a[1] < b[2] and b[1] < a[2] and a[3] < b[4] and b[3] < a[4]


def _contains(a, b):
    return a[1] <= b[1] and b[2] <= a[2] and a[3] <= b[3] and b[4] <= a[4]


class Op:
    __slots__ = ("eng", "fn", "waits", "semkey", "seq", "snap", "has_dep", "val", "is_dma")

    def __init__(self, eng, fn):
        self.eng = eng
        self.fn = fn
        self.waits = []
        self.semkey = None
        self.seq = 0
        self.snap = None
        self.has_dep = False
        self.val = 0
        self.is_dma = False


class Prog:
    ENGS = ("pe", "act", "dve", "pool", "sp")

    def __init__(self, nc, n_dma_sems=40):
        self.nc = nc
        self.ops = {e: [] for e in self.ENGS}
        self.known = {e: {} for e in self.ENGS}
        self.recs = {}
        self.n_dma_sems = n_dma_sems
        self.dma_rr = 0
        self.dma_last = {}

    def _need(self, op, prod):
        if prod is op:
            return
        e = op.eng
        kn = self.known[e]
        if kn.get(prod.semkey, -1) >= prod.seq:
            return
        if (not prod.is_dma) and prod.eng == e:
            if e == "pe" or not SELF_SYNC:
                return
        op.waits.append(prod)
        prod.has_dep = True
        for k, v in prod.snap.items():
            if kn.get(k, -1) < v:
                kn[k] = v
        if kn.get(prod.semkey, -1) < prod.seq:
            kn[prod.semkey] = prod.seq

    def _track(self, op, reads, writes):
        rb = [_box(a) for a in reads]
        wb = [_box(a) for a in writes]
        for b in rb:
            for (ob, oop, ow) in self.recs.get(b[0], ()):
                if ow and _overlap(b, ob):
                    self._need(op, oop)
        for b in wb:
            for (ob, oop, ow) in self.recs.get(b[0], ()):
                if _overlap(b, ob):
                    self._need(op, oop)
        for b in wb:
            lst = self.recs.setdefault(b[0], [])
            lst[:] = [r for r in lst if not _contains(b, r[0])]
            lst.append((b, op, True))
        for b in rb:
            lst = self.recs.setdefault(b[0], [])
            if not op.is_dma:
                lst[:] = [r for r in lst if r[2] or r[1].is_dma or r[1].eng != op.eng
                          or not _contains(b, r[0])]
            lst.append((b, op, False))

    def op(self, eng, fn, reads=(), writes=()):
        o = Op(eng, fn)
        o.semkey = eng
        o.seq = len(self.ops[eng])
        self._track(o, reads, writes)
        o.snap = dict(self.known[eng])
        self.ops[eng].append(o)
        return o

    def dma(self, queue, out, in_, **kw):
        def fn(engobj):
            return engobj.dma_start(out=out, in_=in_, **kw)
        o = Op(queue, fn)
        o.is_dma = True
        idx = self.dma_rr
        self.dma_rr = (self.dma_rr + 1) % self.n_dma_sems
        o.semkey = ("dma", idx)
        prev = self.dma_last.get(idx)
        if prev is not None:
            o.seq = prev.seq + 1
            kn = self.known[queue]
            if kn.get(prev.semkey, -1) < prev.seq:
                o.waits.append(prev)
                for k, v in prev.snap.items():
                    if kn.get(k, -1) < v:
                        kn[k] = v
                kn[prev.semkey] = prev.seq
        self.dma_last[idx] = o
        self._track(o, [in_], [out])
        o.snap = dict(self.known[queue])
        o.has_dep = True
        self.ops[queue].append(o)
        return o

    def emit(self, final_wait_ops=()):
        nc = self.nc
        with contextlib.ExitStack() as st:
            sems = {}
            for e in ("pe", "act", "dve", "pool"):
                sems[e] = st.enter_context(nc.semaphore("s_" + e))
            for i in range(self.n_dma_sems):
                sems[("dma", i)] = st.enter_context(nc.semaphore("s_dma%d" % i))
            for w in final_wait_ops:
                w.has_dep = True
            for e, lst in self.ops.items():
                c = 0
                for o in lst:
                    if o.is_dma:
                        o.val = 16 * (o.seq + 1)
                    elif o.has_dep:
                        c += 1
                        o.val = c
            block = st.enter_context(nc.Block())

            def run(e):
                def body(engobj):
                    for o in self.ops[e]:
                        for w in o.waits:
                            engobj.wait_ge(sems[w.semkey], w.val)
                        ins = o.fn(engobj)
                        if o.is_dma:
                            ins.then_inc(sems[o.semkey], 16)
                        elif o.has_dep:
                            ins.then_inc(sems[o.semkey], 1)
                    if e == "sp":
                        for w in final_wait_ops:
                            engobj.wait_ge(sems[w.semkey], w.val)
                return body

            block.tensor(run("pe"))
            block.scalar(run("act"))
            block.vector(run("dve"))
            block.gpsimd(run("pool"))
            block.sync(run("sp"))


def _isap(x):
    return hasattr(x, "tensor") and hasattr(x, "ap")


def build(n_layers=DEPTH, debug=False, force_ctx=False, stop_after=None):
    nc = bass.Bass("TRN2", target_bir_lowering=False)
    P = Prog(nc)
    st = contextlib.ExitStack()

    def din(name, shape, dt=F32):
        return nc.dram_tensor(name, list(shape), dt, kind="ExternalInput").ap()

    def dscr(name, shape, dt=F32):
        return nc.dram_tensor(name, list(shape), dt, kind="ExternalOutput" if debug else "Internal").ap()

    def sb(name, shape, dt=F32):
        return st.enter_context(nc.sbuf_tensor("sb_" + name, list(shape), dt))

    def ACT(out, in_, func, bias=None, scale=None, accum=None):
        kw = {}
        rd = [in_]
        if bias is not None:
            kw["bias"] = bias
            if _isap(bias):
                rd.append(bias)
        if scale is not None:
            kw["scale"] = scale
            if _isap(scale):
                rd.append(scale)
        wr = [out]
        if accum is not None:
            kw["accum_out"] = accum
            wr.append(accum)
        return P.op("act", lambda e: e.activation(out=out, in_=in_, func=func, **kw), reads=rd, writes=wr)

    def TT(eng, out, in0, in1, op):
        return P.op(eng, lambda e: e.tensor_tensor(out=out, in0=in0, in1=in1, op=op), reads=[in0, in1], writes=[out])

    def STT(eng, out, in0, scalar, in1, op0, op1):
        eng = "dve"
        rd = [in0, in1] + ([scalar] if _isap(scalar) else [])
        return P.op(eng, lambda e: e.scalar_tensor_tensor(out=out, in0=in0, scalar=scalar, in1=in1, op0=op0, op1=op1),
                    reads=rd, writes=[out])

    def TS(eng, out, in0, s1, op0, s2=None, op1=None):
        rd = [in0] + [s for s in (s1, s2) if _isap(s)]
        if op1 is None:
            return P.op(eng, lambda e: e.tensor_scalar(out=out, in0=in0, scalar1=s1, scalar2=None, op0=op0), reads=rd, writes=[out])
        return P.op(eng, lambda e: e.tensor_scalar(out=out, in0=in0, scalar1=s1, scalar2=s2, op0=op0, op1=op1), reads=rd, writes=[out])

    def CP(eng, out, in_):
        if eng == "act":
            return ACT(out, in_, AF.Copy)
        return P.op(eng, lambda e: e.tensor_copy(out=out, in_=in_), reads=[in_], writes=[out])

    def MSET(eng, ap, val):
        return P.op(eng, lambda e: e.memset(ap, val), writes=[ap])

    def RECIP(out, in_):
        return P.op("dve", lambda e: e.reciprocal(out=out, in_=in_), reads=[in_], writes=[out])

    def RSTD(out, ss, scale, eps):
        if scale == 1.0:
            ACT(out, ss, AF.Sqrt, bias=float(eps))
        else:
            ACT(out, ss, AF.Sqrt, bias=float(eps), scale=float(scale))
        RECIP(out, out)

    def MM(out, lhsT, rhs, start, stop, **kw):
        return P.op("pe", lambda e: e.matmul(out, lhsT=lhsT, rhs=rhs, start=start, stop=stop, **kw),
                    reads=[lhsT, rhs], writes=[out])

    def TR(out, in_, ident):
        return P.op("pe", lambda e: e.transpose(out, in_, ident), reads=[in_, ident], writes=[out])

    def SCAN(out, a, b, init):
        rd = [a, b] + ([init] if _isap(init) else [])
        return P.op("dve", lambda e: e.tensor_tensor_scan(out=out, data0=a, data1=b, initial=init, op0=ALU.mult, op1=ALU.add),
                    reads=rd, writes=[out])

    xT_in = din("xT", [D, NT])
    cvec_in = din("cvec", [128, KC, 2])
    ident_in = din("ident", [128, 128])
    masks_in = din("masks", [128, 3, 128])
    ropeM_in = din("ropeM", [2, 128, NT])
    ropeD_in = din("ropeD", [2, 128, NT])
    W = []
    for l in range(n_layers):
        W.append(dict(
            wmod=din(f"wmod{l}", [D, 6 * D]), win=din(f"win{l}", [D, NEXT]), wuq=din(f"wuq{l}", [384, 2048]),
            wukvk=din(f"wukvk{l}", [256, 1024]), wukvv=din(f"wukvv{l}", [256, 1024]),
            lru=din(f"lru{l}", [128, 32, 128]), wbra=din(f"wbra{l}", [D, D]), wbrb=din(f"wbrb{l}", [D, D]),
            wbrc=din(f"wbrc{l}", [D, D]), wout=din(f"wout{l}", [D, D]), wffi=din(f"wffi{l}", [D, 2 * FFH]),
            wffo=din(f"wffo{l}", [FFH, D]), vecs=din(f"vecs{l}", [128, NV]), dlam=din(f"dlam{l}", [1, 256]),
            subg=din(f"subg{l}", [1, 128])))
    outT = nc.dram_tensor("outT", [D, T], F32, kind="ExternalOutput").ap()

    XS = dscr("XS", [D, NT])
    S_RX = dscr("S_RX", [D, NT])
    S_GRG = dscr("S_GRG", [D, NT], BF16)
    S_CQ = dscr("S_CQ", [384, NT])
    S_CKV = dscr("S_CKV", [256, NT])
    S_KR = dscr("S_KR", [128, NT])
    S_DQ = dscr("S_DQ", [8, 128, NT], BF16)
    S_DK = dscr("S_DK", [8, 128, NT], BF16)
    S_MG = dscr("S_MG", [3 * D, NT], BF16)
    S_DV = dscr("S_DV", [NT, D], BF16)
    if debug:
        S_HT = dscr("S_HT", [D, NT], BF16)
        S_UG = dscr("S_UG", [D, NT], BF16)
        S_OB = dscr("S_OB", [D, NT], BF16)
        S_OC = dscr("S_OC", [D, NT], BF16)
        S_MRG = dscr("S_MRG", [D, NT], BF16)
        S_X1 = dscr("S_X1", [D, NT])
        S_MOD = dscr("S_MOD", [128, 48, 2])

    hT = sb("hT", [128, KC, NT], BF16)
    mrg = sb("mrg", [128, KC, NT], BF16)
    ident = sb("identb", [128, 128], BF16)
    masks = sb("masksb", [128, 3, 128], BF16)
    vecs = sb("vecs", [128, NV])
    modT = sb("modT", [128, 48, 2])
    cvec = sb("cvec", [128, KC, 2])
    scb = sb("scb", [128, KC, 2], BF16)
    coef = sb("coef", [128, 6, KC, 2])
    lams = sb("lams", [128, 16])
    lams2 = sb("lams2", [128, 16])
    lamw = sb("lamw", [128, 256])
    lamv = sb("lamv", [128, 4])
    subg = sb("subg", [128, 128])
    NWB = 3
    wbuf = [sb(f"wbuf{i}", [128, KC, 512], BF16) for i in range(NWB)]
    wctr = [0]
    WORK_BYTES = 100 * 1024
    work = sb("work", [128, WORK_BYTES // 4])

    class Carver:
        def __init__(self):
            self.off = 0

        def take(self, shape, dt=F32):
            n = 1
            for s in shape[1:]:
                n *= s
            nbytes = n * (2 if dt == BF16 else 4)
            nwords = (nbytes + 3) // 4
            assert self.off + nwords <= WORK_BYTES // 4, (self.off, nwords)
            v = work[:, self.off:self.off + nwords]
            self.off += nwords
            if dt == BF16:
                v = v.bitcast(BF16)[:, 0:n]
            if len(shape) == 3:
                v = v.rearrange("p (a b) -> p a b", a=shape[1])
            elif len(shape) == 4:
                v = v.rearrange("p (a b c) -> p a b c", a=shape[1], b=shape[2])
            if shape[0] < 128:
                v = v[0:shape[0]]
            return v

    pall = st.enter_context(nc.psum_tensor("pall", [128, 8 * 512], F32))
    pbank = [pall[:, i * 512:(i + 1) * 512] for i in range(8)]

    P.dma("pool", ident[:], ident_in)
    P.dma("pool", masks[:], masks_in)
    P.dma("sp", cvec[:], cvec_in)
    ACT(scb[:], cvec[:], AF.Silu)
    P.dma("sp", XS, xT_in)

    ones_l = masks[:, 0, :]
    m96 = masks[:, 1, 0:96]
    b64 = masks[:, 2, :]

    def load_w(dst, src):
        return P.dma("pool", dst, src)

    def next_wbuf():
        b = wbuf[wctr[0] % NWB]
        wctr[0] += 1
        return b

    def gemm_fm(wsrc, kc_n, act, units, tblocks, evac, banks):
        bi = 0
        i = 0
        while i < len(units):
            lo = min(c[0] for c in units[i])
            j = i
            hi = lo
            while j < len(units):
                uhi = max(c[0] + c[1] for c in units[j])
                if uhi - lo > 512:
                    break
                hi = uhi
                j += 1
            assert j > i
            wt = next_wbuf()
            ncol = hi - lo
            load_w(wt[:, 0:kc_n, 0:ncol], wsrc[:, lo:hi].rearrange("(k p) n -> p k n", p=128))
            for u in units[i:j]:
                for (t0, n) in tblocks:
                    pss = []
                    for ch in u:
                        c0, m = ch[0], ch[1]
                        prow = ch[2] if len(ch) > 2 else 0
                        pb = banks[bi % len(banks)]
                        bi += 1
                        for k in range(kc_n):
                            MM(pb[prow:prow + m, 0:n], wt[:, k, c0 - lo:c0 - lo + m], act[:, k, t0:t0 + n], k == 0, k == kc_n - 1)
                        pss.append(pb)
                    evac(u, t0, n, pss)
            i = j

    fin_ops = []
    for l in range(n_layers):
        w = W[l]
        last = (l == n_layers - 1) and not force_ctx
        lam_init = 0.8 - 0.6 * float(np.exp(-0.3 * l))
        P.dma("sp", vecs[:], w["vecs"])
        P.dma("sp", lamw[:], w["dlam"].broadcast_to([128, 256]))
        P.dma("sp", subg[:], w["subg"].broadcast_to([128, 128]))
        TS("dve", subg[:], subg[:], 1.0 - lam_init, ALU.mult)
        cv = Carver()
        ltmp = cv.take([128, 128])
        lsum = cv.take([128, 2])
        TT("dve", ltmp[:, 0:64], lamw[:, 0:64], lamw[:, 64:128], ALU.mult)
        TT("dve", ltmp[:, 64:128], lamw[:, 128:192], lamw[:, 192:256], ALU.mult)
        P.op("dve", lambda e: e.reduce_sum(out=lsum[:, 0:1], in_=ltmp[:, 0:64], axis=mybir.AxisListType.X),
             reads=[ltmp[:, 0:64]], writes=[lsum[:, 0:1]])
        P.op("dve", lambda e: e.reduce_sum(out=lsum[:, 1:2], in_=ltmp[:, 64:128], axis=mybir.AxisListType.X),
             reads=[ltmp[:, 64:128]], writes=[lsum[:, 1:2]])
        ACT(lsum[:], lsum[:], AF.Exp)
        TT("dve", lamv[:, 1:2], lsum[:, 1:2], lsum[:, 0:1], ALU.subtract)
        TS("dve", lamv[:, 0:1], lamv[:, 1:2], -lam_init, ALU.add)
        ACT(lams2[:], vecs[:, V_LAM:V_LAM + 16], AF.Exp, scale=-1.0)
        ACT(lams2[:], lams2[:], AF.Ln, bias=1.0)
        TS("dve", lams[:], lams2[:], -8.0, ALU.mult)
        TS("dve", lams2[:], lams2[:], -16.0, ALU.mult)

        def evac_mod(u, t0, n, pss):
            ch = u[0][0] // 128
            TS("dve", modT[:, ch, :], pss[0][:, 0:2], vecs[:, V_BMOD + ch:V_BMOD + ch + 1], ALU.add)
        gemm_fm(w["wmod"], KC, scb, [[(c * 128, 128)] for c in range(48)], [(0, 2)], evac_mod, pbank[0:2])
        if debug and l == 0:
            P.dma("sp", S_MOD, modT[:])
        for j in range(2):
            for (ia, ib, ig, base, gv) in ((0, 1, 2, 0, V_N1G), (3, 4, 5, 24, V_N2G)):
                TS("dve", coef[:, ia, :, j], modT[:, base + 8:base + 16, j], 1.0, ALU.add, 32.0, ALU.mult)
                TT("dve", coef[:, ia, :, j], coef[:, ia, :, j], vecs[:, gv:gv + 8], ALU.mult)
                CP("dve", coef[:, ib, :, j], modT[:, base:base + 8, j])
                CP("dve", coef[:, ig, :, j], modT[:, base + 16:base + 24, j])

        def norm_block(xb, t0, n, ia, ib, cvn, bank):
            j = 0 if t0 < T else 1
            sq = cvn["sq"]
            for k in range(KC):
                ACT(sq[:, k, 0:n], xb[:, k, 0:n], AF.Square)
            for k in range(KC):
                MM(bank[:, 0:n], ones_l, sq[:, k, 0:n], k == 0, k == KC - 1)
            rs = cvn["rs"]
            RSTD(rs[:, 0:n], bank[:, 0:n], 1.0, float(D * EPS))
            tmp = cvn["tmp"]
            for k in range(KC):
                STT("dve" if k % 2 == 0 else "pool", tmp[:, k, 0:n], xb[:, k, 0:n], coef[:, ia, k, j:j + 1], rs[:, 0:n], ALU.mult, ALU.mult)
                ACT(hT[:, k, t0:t0 + n], tmp[:, k, 0:n], AF.Identity, bias=coef[:, ib, k, j:j + 1])

        cv = Carver()
        cvn = dict(sq=cv.take([128, KC, 512], BF16), rs=cv.take([128, 512]), tmp=cv.take([128, KC, 512]))
        xbs = [cv.take([128, KC, 512]) for _ in range(2)]
        for bi_, (t0, n) in enumerate(TBLK):
            xb = xbs[bi_ % 2]
            P.dma("sp", xb[:, :, 0:n], XS[:, t0:t0 + n].rearrange("(k p) t -> p k t", p=128))
            norm_block(xb, t0, n, 0, 1, cvn, pbank[6])
        if debug and l == 0:
            P.dma("sp", S_HT.rearrange("(k p) t -> p k t", p=128), hT[:])

        if stop_after == "norm":
            break
        cv = Carver()
        ropeD = cv.take([128, 2, NT])
        P.dma("sp", ropeD, ropeD_in.rearrange("a p t -> p a t"))
        stg = [cv.take([128, NT]) for _ in range(2)]
        stgb = [cv.take([128, NT], BF16) for _ in range(2)]
        sctr = [0]
        wk = dict(sq=cv.take([128, 512], BF16), rs=cv.take([128, 512]), t1=cv.take([128, 512]), t2=cv.take([128, 512]))

        def stage(bf):
            s = (stgb if bf else stg)[sctr[0] % 2]
            return s

        def evac_win(u, t0, n, pss):
            c0 = u[0][0]
            lastb = (t0 + n == NT)
            if c0 < O_RG:
                s = stg[(c0 // 128) % 2]
                CP("act", s[:, t0:t0 + n], pss[0][:, 0:n])
                if lastb:
                    P.dma("sp", S_RX[c0:c0 + 128, :], s[:])
            elif c0 < O_CQ:
                r0 = c0 - O_RG
                s = stgb[(c0 // 128) % 2]
                ACT(wk["t1"][:, 0:n], pss[0][:, 0:n], AF.Square)
                TS("dve", wk["t1"][:, 0:n], wk["t1"][:, 0:n], 0.044715, ALU.mult, 1.0, ALU.add)
                TT("dve", wk["t1"][:, 0:n], wk["t1"][:, 0:n], pss[0][:, 0:n], ALU.mult)
                ACT(wk["t1"][:, 0:n], wk["t1"][:, 0:n], AF.Sigmoid, scale=1.5957691216057308)
                TT("dve", s[:, t0:t0 + n], wk["t1"][:, 0:n], pss[0][:, 0:n], ALU.mult)
                if lastb:
                    P.dma("sp", S_GRG[r0:r0 + 128, :], s[:])
            elif c0 < O_KR:
                s = stg[(c0 // 128) % 2]
                CP("act", s[:, t0:t0 + n], pss[0][:, 0:n])
                if lastb:
                    if c0 < O_CKV:
                        P.dma("sp", S_CQ[c0 - O_CQ:c0 - O_CQ + 128, :], s[:])
                    else:
                        P.dma("sp", S_CKV[c0 - O_CKV:c0 - O_CKV + 128, :], s[:])
            elif c0 < O_DQ:
                s = stg[(c0 // 128) % 2]
                CP("act", s[64:128, t0:t0 + n], pss[0][64:128, 0:n])
                if lastb:
                    P.dma("sp", S_KR[64:128, :], s[64:128, :])
            elif c0 < O_MG:
                isk = c0 >= O_DK
                hh = (c0 - (O_DK if isk else O_DQ)) // 256
                vg = V_GDK if isk else V_GDQ
                s = stgb[hh % 2]
                po, psw = pss
                ACT(wk["sq"][:, 0:n], po[:, 0:n], AF.Square)
                MM(pbank[6][:, 0:n], b64, wk["sq"][:, 0:n], True, True)
                RSTD(wk["rs"][:, 0:n], pbank[6][:, 0:n], 1.0 / 64, EPS)
                STT("dve", wk["t1"][:, 0:n], po[:, 0:n], vecs[:, vg:vg + 1], ropeD[:, 0, t0:t0 + n], ALU.mult, ALU.mult)
                STT("dve", wk["t2"][:, 0:n], psw[:, 0:n], vecs[:, vg + 1:vg + 2], ropeD[:, 1, t0:t0 + n], ALU.mult, ALU.mult)
                TT("dve", wk["t1"][:, 0:n], wk["t1"][:, 0:n], wk["t2"][:, 0:n], ALU.add)
                TT("dve", s[:, t0:t0 + n], wk["t1"][:, 0:n], wk["rs"][:, 0:n], ALU.mult)
                if lastb:
                    P.dma("sp", (S_DK if isk else S_DQ)[hh], s[:])
            else:
                r0 = c0 - O_MG
                s = stgb[(c0 // 128) % 2]
                ACT(s[:, t0:t0 + n], pss[0][:, 0:n], AF.Sigmoid)
                if lastb:
                    P.dma("sp", S_MG[r0:r0 + 128, :], s[:])

        units = [[(c, 128)] for c in range(O_RX, O_KR, 128)]
        units.append([(O_KR, 64, 64)])
        gemm_fm(w["win"], KC, hT, units, TBLK, evac_win, pbank[0:4])
        if stop_after == "win1":
            break
        units = [[(c, 128), (c + 128, 128)] for c in range(O_DQ, O_MG, 256)]
        gemm_fm(w["win"], KC, hT, units, TBLK, evac_win, pbank[0:4])
        if stop_after == "win2":
            break
        units = [[(c, 128)] for c in range(O_MG, O_DV, 128)]
        gemm_fm(w["win"], KC, hT, units, TBLK, evac_win, pbank[0:4])
        if stop_after == "win3":
            break
        wdv = [next_wbuf(), next_wbuf()]
        for i_ in range(2):
            load_w(wdv[i_][:], w["win"][:, O_DV + i_ * 512:O_DV + (i_ + 1) * 512].rearrange("(k p) n -> p k n", p=128))
        for tt in range(NT // 128):
            s = stgb[tt % 2]
            for i_ in range(2):
                pb = pbank[(tt * 2 + i_) % 4]
                for k in range(KC):
                    MM(pb[:, :], hT[:, k, tt * 128:(tt + 1) * 128], wdv[i_][:, k, :], k == 0, k == KC - 1)
                CP("act" if i_ == 0 else "dve", s[:, i_ * 512:(i_ + 1) * 512], pb[:, :])
            P.dma("sp", S_DV[tt * 128:(tt + 1) * 128, :], s[:, 0:1024])

        if stop_after == "win":
            break
        cv = Carver()
        brT = hT
        LP = 2051
        xpad = cv.take([128, 2310])
        xc = cv.take([128, NT])
        xcb = cv.take([128, NT], BF16)
        g_r = cv.take([128, NT])
        g_i = cv.take([128, NT])
        av = g_r
        bv = cv.take([128, NT])
        lruw = cv.take([128, 32, 128], BF16)
        load_w(lruw, w["lru"])
        hf = cv.take([128, NT])
        hb = cv.take([128, NT])
        grg = cv.take([128, NT], BF16)
        MSET("pool", xpad[:, :], 0.0)
        for j in range(KC):
            P.dma("sp", xpad[:, 2:2 + T], S_RX[j * 128:(j + 1) * 128, 0:T])
            P.dma("sp", xpad[:, LP + 2:LP + 2 + C], S_RX[j * 128:(j + 1) * 128, T:NT])
            P.dma("sp", grg[:], S_GRG[j * 128:(j + 1) * 128, :])
            cw = lambda tap: vecs[:, V_CONVW + tap * 8 + j:V_CONVW + tap * 8 + j + 1]
            for (o0, base, n) in ((0, 0, T), (T, LP, C)):
                ACT(xc[:, o0:o0 + n], xpad[:, base:base + n], AF.Identity, bias=vecs[:, V_CONVB + j:V_CONVB + j + 1], scale=cw(0))
                for tap in (1, 2, 3):
                    STT("dve", xc[:, o0:o0 + n], xpad[:, base + tap:base + tap + n], cw(tap), xc[:, o0:o0 + n], ALU.mult, ALU.add)
            CP("act", xcb[:], xc[:])
            for r in range(2):
                for gi, (gdst, vb_) in enumerate(((g_r, V_BA), (g_i, V_BI))):
                    wsl = lruw[:, gi * 16 + r * 8 + j, :]
                    for bi_, (t0, n) in enumerate(TBLK):
                        pb = pbank[bi_ % 4]
                        MM(pb[:, 0:n], wsl, xcb[:, t0:t0 + n], True, True)
                        ACT(gdst[:, t0:t0 + n], pb[:, 0:n], AF.Sigmoid, bias=vecs[:, vb_ + r * 8 + j:vb_ + r * 8 + j + 1])
                ACT(bv[:], g_r[:], AF.Exp, scale=lams2[:, r * 8 + j:r * 8 + j + 1])
                ACT(av[:], g_r[:], AF.Exp, scale=lams[:, r * 8 + j:r * 8 + j + 1])
                ACT(bv[:], bv[:], AF.Sqrt, scale=-1.0, bias=1.0)
                TT("dve", bv[:], bv[:], g_i[:], ALU.mult)
                TT("dve", bv[:], bv[:], xc[:], ALU.mult)
                if r == 0:
                    SCAN(hf[:, T:NT], av[:, T:NT], bv[:, T:NT], 0.0)
                    SCAN(hf[:, 0:T], av[:, 0:T], bv[:, 0:T], hf[:, NT - 1:NT])
                else:
                    SCAN(hb[:, T:NT][:, ::-1], av[:, T:NT][:, ::-1], bv[:, T:NT][:, ::-1], 0.0)
                    SCAN(hb[:, 0:T][:, ::-1], av[:, 0:T][:, ::-1], bv[:, 0:T][:, ::-1], hb[:, T:T + 1])
            TT("pool", hf[:, 0:1152], hf[:, 0:1152], hb[:, 0:1152], ALU.add)
            TT("dve", hf[:, 1152:NT], hf[:, 1152:NT], hb[:, 1152:NT], ALU.add)
            TT("pool", brT[:, j, 0:1152], hf[:, 0:1152], grg[:, 0:1152], ALU.mult)
            TT("dve", brT[:, j, 1152:NT], hf[:, 1152:NT], grg[:, 1152:NT], ALU.mult)
        if debug and l == 0:
            P.dma("sp", S_UG.rearrange("(k p) t -> p k t", p=128), brT[:])

        def branch_merge(wsrc, bidx, cvb):
            gts = cvb["gts"]
            tmpb = cvb["tmpb"]

            def evac_br(u, t0, n, pss):
                o = u[0][0] // 128
                g = gts[0]
                if t0 == 0:
                    P.dma("sp", g[:], S_MG[bidx * D + o * 128:bidx * D + (o + 1) * 128, :])
                if bidx == 0:
                    TT("dve", mrg[:, o, t0:t0 + n], pss[0][:, 0:n], g[:, t0:t0 + n], ALU.mult)
                else:
                    TT("dve", tmpb[:, 0:n], pss[0][:, 0:n], g[:, t0:t0 + n], ALU.mult)
                    TT("pool", mrg[:, o, t0:t0 + n], mrg[:, o, t0:t0 + n], tmpb[:, 0:n], ALU.add)
            gemm_fm(wsrc, KC, brT, [[(c * 128, 128)] for c in range(KC)], LBLK, evac_br, pbank[0:4])

        LBLK = TBLK[0:4] if last else TBLK
        cvb = dict(gts=[cv.take([128, NT], BF16)], tmpb=cv.take([128, 512]))
        branch_merge(w["wbra"], 0, cvb)

        if stop_after == "A":
            break
        cv = Carver()
        cvb = dict(gts=[cv.take([128, NT], BF16)], tmpb=cv.take([128, 512]))
        ropeM = cv.take([128, NT])
        P.dma("sp", ropeM, ropeM_in[0])
        cqn = cv.take([128, 3, NT], BF16)
        ckvn = cv.take([128, 2, NT], BF16)
        krr = cv.take([128, NT])
        sqk = cv.take([128, NT], BF16)
        QT = [cv.take([128, NT], BF16) for _ in range(2)]
        KT = [cv.take([128, NT], BF16) for _ in range(2)]
        Vh = [cv.take([128, 18, 65], BF16) for _ in range(2)]
        Otm = cv.take([128, 18, 128], BF16)
        PT2 = [cv.take([128, 2, 512], BF16) for _ in range(3)]
        PT = [PT2[0][:, 0, :], PT2[0][:, 1, :], PT2[1][:, 0, :]]
        ld_ = cv.take([128, 3, 512])
        wk = dict(sq=cv.take([128, 512], BF16), rs=cv.take([128, 512]), rs2=ld_[:, 0, :], t1=ld_[:, 1, :], t2=ld_[:, 2, :],
                  ld=ld_, rc=cv.take([128, 4]))
        ptrM = pbank[6].bitcast(BF16)
        MSET("pool", sqk[:, :], 0.0)
        for v_ in Vh:
            MSET("pool", v_[:, :, 64:65], 1.0)
        for (src, nch, dst, gv) in ((S_CQ, 3, cqn, V_QNG), (S_CKV, 2, ckvn, V_KVNG)):
            for (t0, n) in TBLK:
                ld = wk["ld"]
                P.dma("sp", ld[:, 0:nch, 0:n], src[:, t0:t0 + n].rearrange("(k p) t -> p k t", p=128))
                for k in range(nch):
                    ACT(PT[k][:, 0:n], ld[:, k, 0:n], AF.Square)
                for k in range(nch):
                    MM(pbank[6][:, 0:n], ones_l, PT[k][:, 0:n], k == 0, k == nch - 1)
                RSTD(wk["rs"][:, 0:n], pbank[6][:, 0:n], 1.0 / (nch * 128), EPS)
                for k in range(nch):
                    STT("dve", dst[:, k, t0:t0 + n], ld[:, k, 0:n], vecs[:, gv + k:gv + k + 1], wk["rs"][:, 0:n], ALU.mult, ALU.mult)
        for (t0, n) in TBLK:
            ld = wk["ld"]
            P.dma("sp", ld[64:128, 0, 0:n], S_KR[64:128, t0:t0 + n])
            ACT(sqk[64:96, t0:t0 + n], ld[64:96, 0, 0:n], AF.Square)
            STT("dve", krr[64:96, t0:t0 + n], ld[64:96, 0, 0:n], vecs[64:96, V_GK:V_GK + 1], ropeM[64:96, t0:t0 + n], ALU.mult, ALU.mult)
            CP("act", wk["t2"][64:96, 0:n], ld[96:128, 0, 0:n])
            P.dma("sp", wk["t1"][64:96, 0:n], ropeM_in[1, 64:96, t0:t0 + n])
            TT("dve", wk["t2"][64:96, 0:n], wk["t2"][64:96, 0:n], wk["t1"][64:96, 0:n], ALU.mult)
            STT("dve", krr[64:96, t0:t0 + n], wk["t2"][64:96, 0:n], vecs[64:96, V_GKSW:V_GKSW + 1], krr[64:96, t0:t0 + n], ALU.mult, ALU.add)
        wq = next_wbuf()
        wq2 = next_wbuf()
        wkk = next_wbuf()
        wqv = [b_.rearrange("p k n -> p (k n)")[:, 0:3072].rearrange("p (k n) -> p k n", k=3) for b_ in (wq, wq2)]
        for i_ in range(2):
            load_w(wqv[i_], w["wuq"][:, i_ * 1024:(i_ + 1) * 1024].rearrange("(k p) n -> p k n", p=128))
        wkv = wkk.rearrange("p k n -> p (k n)").rearrange("p (a k n) -> p a k n", a=2, k=2)
        load_w(wkv[:, 0], w["wukvk"].rearrange("(k p) n -> p k n", p=128))
        load_w(wkv[:, 1], w["wukvv"].rearrange("(k p) n -> p k n", p=128))

        QBLK = [(0, 512, 0, 18), (512, 512, 0, 18), (1024, 512, 0, 18), (1536, 512, 0, 18), (2048, 256, 16, 18)]
        if last:
            QBLK = QBLK[0:4]
        def mla_prep_steps(hd):
            Qh, Kh, V_ = QT[hd % 2], KT[hd % 2], Vh[hd % 2]
            wqh = wqv[hd // 8][:, :, (hd % 8) * 128:(hd % 8 + 1) * 128]
            steps = []
            for bi_, (t0, n) in enumerate(TBLK):
                def qstep(t0=t0, n=n):
                    pq = pbank[6]
                    for k in range(3):
                        MM(pq[:, 0:n], wqh[:, k, :], cqn[:, k, t0:t0 + n], k == 0, k == 2)
                    ACT(wk["sq"][:, 0:n], pq[:, 0:n], AF.Square)
                    MM(pbank[7][0:96, 0:n], m96, wk["sq"][:, 0:n], True, True)
                    RSTD(wk["rs"][0:96, 0:n], pbank[7][0:96, 0:n], 1.0 / 96, EPS)
                    TS("dve", wk["t1"][0:64, 0:n], pq[0:64, 0:n], vecs[0:64, V_GQ:V_GQ + 1], ALU.mult)
                    STT("dve", wk["t1"][64:96, 0:n], pq[64:96, 0:n], vecs[64:96, V_GQ:V_GQ + 1], ropeM[64:96, t0:t0 + n], ALU.mult, ALU.mult)
                    TT("dve", wk["t2"][64:96, 0:n], pq[96:128, 0:n], ropeM[0:32, t0:t0 + n], ALU.mult)
                    STT("dve", wk["t1"][64:96, 0:n], wk["t2"][64:96, 0:n], vecs[64:96, V_GQSW:V_GQSW + 1], wk["t1"][64:96, 0:n], ALU.mult, ALU.add)
                    TT("pool", Qh[0:96, t0:t0 + n], wk["t1"][0:96, 0:n], wk["rs"][0:96, 0:n], ALU.mult)
                steps.append(qstep)

                def kstep(t0=t0, n=n):
                    pk = pbank[6]
                    for k in range(2):
                        MM(pk[0:64, 0:n], wkv[:, 0, k, hd * 64:(hd + 1) * 64], ckvn[:, k, t0:t0 + n], k == 0, k == 1)
                    ACT(sqk[0:64, t0:t0 + n], pk[0:64, 0:n], AF.Square)
                    MM(pbank[7][0:96, 0:n], m96, sqk[:, t0:t0 + n], True, True)
                    RSTD(wk["rs2"][0:96, 0:n], pbank[7][0:96, 0:n], 1.0 / 96, EPS)
                    STT("dve", Kh[0:64, t0:t0 + n], pk[0:64, 0:n], vecs[0:64, V_GK:V_GK + 1], wk["rs2"][0:64, 0:n], ALU.mult, ALU.mult)
                    TT("pool", Kh[64:96, t0:t0 + n], krr[64:96, t0:t0 + n], wk["rs2"][64:96, 0:n], ALU.mult)
                steps.append(kstep)
            for g4 in range(0, 18, 4):
                def vstep(g4=g4):
                    nt_ = min(4, 18 - g4)
                    pv = pbank[6]
                    for i_ in range(nt_):
                        tt = g4 + i_
                        for k in range(2):
                            MM(pv[:, i_ * 64:(i_ + 1) * 64], ckvn[:, k, tt * 128:(tt + 1) * 128], wkv[:, 1, k, hd * 64:(hd + 1) * 64], k == 0, k == 1)
                    CP("dve", V_[:, g4:g4 + nt_, 0:64], pv[:, 0:nt_ * 64].rearrange("p (a b) -> p a b", a=nt_))
                steps.append(vstep)
            return steps

        for s_ in mla_prep_steps(0):
            s_()
        gpair = 0
        for hd in range(16):
            Qh, Kh, V_ = QT[hd % 2], KT[hd % 2], Vh[hd % 2]
            nxt = mla_prep_steps(hd + 1) if hd + 1 < 16 else []
            pairs = [(qi, kt) for qi, (q0, qn, k0, k1) in enumerate(QBLK) for kt in range(k0, k1, 2)]
            n_p = len(pairs)

            def stA(j):
                qi, kt = pairs[j]
                q0, qn, k0, k1 = QBLK[qi]
                b0 = ((gpair + j) % 2) * 2
                for h2 in range(2):
                    MM(pbank[b0 + h2][:, 0:qn], Kh[0:96, (kt + h2) * 128:(kt + h2 + 1) * 128], Qh[0:96, q0:q0 + qn], True, True)
                src = pall[:, b0 * 512:(b0 + 2) * 512].rearrange("p (a b) -> p a b", a=2)[:, :, 0:qn]
                ACT(PT2[(gpair + j) % 3][:, :, 0:qn], src, AF.Exp, scale=MLA_SCALE)

            def stB(j):
                qi, kt = pairs[j]
                q0, qn, k0, k1 = QBLK[qi]
                nq = qn // 128
                po = pbank[4 + qi % 2]
                pt_ = PT2[(gpair + j) % 3]
                for h2 in range(2):
                    for qt in range(nq):
                        MM(po[:, qt * 65:(qt + 1) * 65], pt_[:, h2, qt * 128:(qt + 1) * 128], V_[:, kt + h2, :],
                           (kt + h2 == k0 and qt == 0), (kt + h2 == k1 - 1 and qt == nq - 1), skip_group_check=True)
                if kt + 2 == k1:
                    pov = po[:, 0:nq * 65].rearrange("p (a b) -> p a b", a=nq)
                    RECIP(wk["rc"][:, 0:nq], pov[:, :, 64])
                    tq0 = q0 // 128
                    TT("dve", Otm[:, tq0:tq0 + nq, (hd % 2) * 64:(hd % 2) * 64 + 64], pov[:, :, 0:64],
                       wk["rc"][:, 0:nq].unsqueeze(2).broadcast_to([128, nq, 64]), ALU.mult)

            for j in range(n_p + 1):
                if j < n_p:
                    stA(j)
                if j >= 1:
                    stB(j - 1)
                if j % 2 == 1 and nxt:
                    nxt.pop(0)()
            while nxt:
                nxt.pop(0)()
            gpair += n_p
            if hd % 2 == 1:
                for g4 in range(0, 18, 4):
                    nt_ = min(4, 18 - g4)
                    for i_ in range(nt_):
                        TR(ptrM[:, i_ * 128:(i_ + 1) * 128], Otm[:, g4 + i_, :], ident[:])
                    CP("dve", brT[:, hd // 2, g4 * 128:(g4 + nt_) * 128], ptrM[:, 0:nt_ * 128])
        if debug and l == 0:
            P.dma("sp", S_OB.rearrange("(k p) t -> p k t", p=128), brT[:])
        branch_merge(w["wbrb"], 1, cvb)

        if stop_after == "mla":
            break
        cv = Carver()
        cvb = dict(gts=[cv.take([128, NT], BF16)], tmpb=cv.take([128, 512]))
        QT = [cv.take([128, NT], BF16) for _ in range(2)]
        KT = [cv.take([128, NT], BF16) for _ in range(2)]
        Vd = [cv.take([128, 18, 129], BF16) for _ in range(2)]
        PT2 = [cv.take([128, 2, 512], BF16) for _ in range(3)]
        ptrD = pbank[0].bitcast(BF16)
        Otm = cv.take([128, 18, 128], BF16)
        wk = dict(rc=cv.take([128, 8]), o=cv.take([128, 4, 128]), t=cv.take([128, 4, 128]), ss=cv.take([128, 4]), junk=cv.take([128, 128]),
                  oraw=cv.take([128, 4, 258]))
        for v_ in Vd:
            MSET("pool", v_[:, :, 128:129], 1.0)
        gpair = 0
        for hd in range(8):
            Qh, Kh, V_ = QT[hd % 2], KT[hd % 2], Vd[hd % 2]
            P.dma("sp", Qh[:], S_DQ[hd])
            P.dma("sp", Kh[:], S_DK[hd])
            P.dma("sp", V_[:, :, 0:128], S_DV[:, hd * 128:(hd + 1) * 128].rearrange("(a p) d -> p a d", p=128))
            pairs = [(qi, kt) for qi, (q0, qn, k0, k1) in enumerate(QBLK) for kt in range(k0, k1)]
            n_p = len(pairs)

            def stA(j):
                qi, kt = pairs[j]
                q0, qn, k0, k1 = QBLK[qi]
                b0 = ((gpair + j) % 2) * 2
                for m in range(2):
                    r0 = m * 64
                    MM(pbank[b0 + m][:, 0:qn], Kh[r0:r0 + 64, kt * 128:(kt + 1) * 128], Qh[r0:r0 + 64, q0:q0 + qn], True, True)
                src = pall[:, b0 * 512:(b0 + 2) * 512].rearrange("p (a b) -> p a b", a=2)[:, :, 0:qn]
                ACT(PT2[(gpair + j) % 3][:, :, 0:qn], src, AF.Exp, scale=DIFF_SCALE)

            def stB(j):
                qi, kt = pairs[j]
                q0, qn, k0, k1 = QBLK[qi]
                nq = qn // 128
                pt_ = PT2[(gpair + j) % 3]
                for m in range(2):
                    for qt in range(nq):
                        ob, oo = pbank[4 + m * 2 + qt // 2], (qt % 2) * 129
                        MM(ob[:, oo:oo + 129], pt_[:, m, qt * 128:(qt + 1) * 128], V_[:, kt, :],
                           (kt == k0 and qt % 2 == 0), (kt == k1 - 1 and qt % 2 == 1), skip_group_check=True)
                if kt == k1 - 1:
                    oraw = wk["oraw"]
                    used = [0, 1, 2, 3] if nq == 4 else [0, 2]
                    for ci, b_ in enumerate(used):
                        CP("dve", oraw[:, b_, :], pbank[4 + b_][:, 0:258])
                    tq0 = q0 // 128
                    for qt in range(nq):
                        ob0, o0 = oraw[:, qt // 2, :], (qt % 2) * 129
                        ob1, o1 = oraw[:, 2 + qt // 2, :], (qt % 2) * 129
                        RECIP(wk["rc"][:, 0:1], ob0[:, o0 + 128:o0 + 129])
                        RECIP(wk["rc"][:, 1:2], ob1[:, o1 + 128:o1 + 129])
                        TT("dve", wk["rc"][:, 1:2], wk["rc"][:, 1:2], lamv[:, 0:1], ALU.mult)
                        TS("pool", wk["t"][:, qt, :], ob1[:, o1:o1 + 128], wk["rc"][:, 1:2], ALU.mult)
                        STT("dve", wk["o"][:, qt, :], ob0[:, o0:o0 + 128], wk["rc"][:, 0:1], wk["t"][:, qt, :], ALU.mult, ALU.add)
                        ACT(wk["junk"][:, :], wk["o"][:, qt, :], AF.Square, accum=wk["ss"][:, qt:qt + 1])
                    RSTD(wk["ss"][:, 0:nq], wk["ss"][:, 0:nq], 1.0 / 128, EPS)
                    for qt in range(nq):
                        STT("dve", Otm[:, tq0 + qt, :], wk["o"][:, qt, :], wk["ss"][:, qt:qt + 1], subg[:], ALU.mult, ALU.mult)

            for j in range(n_p + 1):
                if j < n_p:
                    stA(j)
                if j >= 1:
                    stB(j - 1)
            gpair += n_p
            for g4 in range(0, 18, 4):
                nt_ = min(4, 18 - g4)
                for i_ in range(nt_):
                    TR(ptrD[:, i_ * 128:(i_ + 1) * 128], Otm[:, g4 + i_, :], ident[:])
                CP("dve", brT[:, hd, g4 * 128:(g4 + nt_) * 128], ptrD[:, 0:nt_ * 128])
        if debug and l == 0:
            P.dma("sp", S_OC.rearrange("(k p) t -> p k t", p=128), brT[:])
        branch_merge(w["wbrc"], 2, cvb)
        if debug and l == 0:
            P.dma("sp", S_MRG.rearrange("(k p) t -> p k t", p=128), mrg[:])

        if stop_after == "diff":
            break
        cv = Carver()
        cvn = dict(sq=cv.take([128, KC, 512], BF16), rs=cv.take([128, 512]), tmp=cv.take([128, KC, 512]))
        xbs = [cv.take([128, KC, 512]) for _ in range(2)]
        wo = [next_wbuf(), next_wbuf()]
        for i_ in range(2):
            load_w(wo[i_][:], w["wout"][:, i_ * 512:(i_ + 1) * 512].rearrange("(k p) n -> p k n", p=128))
        oblks = TBLK if not last else TBLK[0:4]
        for bi_, (t0, n) in enumerate(oblks):
            xb = xbs[bi_ % 2]
            j = 0 if t0 < T else 1
            P.dma("sp", xb[:, :, 0:n], XS[:, t0:t0 + n].rearrange("(k p) t -> p k t", p=128))
            for o in range(KC):
                pb = pbank[o % 4]
                for k in range(KC):
                    MM(pb[:, 0:n], wo[o // 4][:, k, (o % 4) * 128:(o % 4 + 1) * 128], mrg[:, k, t0:t0 + n], k == 0, k == KC - 1)
                STT("dve", xb[:, o, 0:n], pb[:, 0:n], coef[:, 2, o, j:j + 1], xb[:, o, 0:n], ALU.mult, ALU.add)
            P.dma("sp", XS[:, t0:t0 + n].rearrange("(k p) t -> p k t", p=128), xb[:, :, 0:n])
            if debug and l == 0:
                P.dma("sp", S_X1[:, t0:t0 + n].rearrange("(k p) t -> p k t", p=128), xb[:, :, 0:n])
            norm_block(xb, t0, n, 3, 4, cvn, pbank[6])

        cv = Carver()
        if last:
            groups = [[(0, 512, 0), (512, 512, 512)], [(1024, 512, 0), (1536, 512, 512)]]
        else:
            groups = [[(0, 512, 0), (512, 512, 512), (2048, 256, 1024)], [(1024, 512, 0), (1536, 512, 512)]]
        half = 1280 if not last else 1024
        actT = cv.take([128, 22, half], BF16)
        gbuf = [cv.take([128, 512]) for _ in range(2)]
        xio = [cv.take([128, 512]) for _ in range(3)]
        xctr = [0]
        wfo = cv.take([128, 22, 512], BF16)
        for grp in range(2):
            blks = [(a_, b_) for (a_, b_, _) in groups[grp]]
            aoff = {a_: c_ for (a_, _, c_) in groups[grp]}

            def evac_ffi(u, t0, n, pss, aoff=aoff):
                f = u[0][0] // 256
                gb = gbuf[f % 2]
                ACT(gb[:, 0:n], pss[0][:, 0:n], AF.Silu)
                TT("dve", actT[:, f, aoff[t0]:aoff[t0] + n], pss[1][:, 0:n], gb[:, 0:n], ALU.mult)
            gemm_fm(w["wffi"], KC, hT, [[(f * 256, 128), (f * 256 + 128, 128)] for f in range(22)], blks, evac_ffi, pbank[0:4])
            for oc in range(2):
                load_w(wfo[:], w["wffo"][:, oc * 512:(oc + 1) * 512].rearrange("(k p) n -> p k n", p=128))
                for o4 in range(4):
                    o = oc * 4 + o4
                    for (t0, n) in blks:
                        j = 0 if t0 < T else 1
                        xb = xio[xctr[0] % 3]
                        xctr[0] += 1
                        P.dma("sp", xb[:, 0:n], XS[o * 128:(o + 1) * 128, t0:t0 + n])
                        pb = pbank[4 + (xctr[0] % 2)]
                        for k in range(22):
                            MM(pb[:, 0:n], wfo[:, k, o4 * 128:(o4 + 1) * 128], actT[:, k, aoff[t0]:aoff[t0] + n], k == 0, k == 21)
                        STT("dve", xb[:, 0:n], pb[:, 0:n], coef[:, 5, o, j:j + 1], xb[:, 0:n], ALU.mult, ALU.add)
                        if last:
                            fin_ops.append(P.dma("sp", outT[o * 128:(o + 1) * 128, t0:t0 + n], xb[:, 0:n]))
                        else:
                            P.dma("sp", XS[o * 128:(o + 1) * 128, t0:t0 + n], xb[:, 0:n])

    P.emit(final_wait_ops=fin_ops + (list(P.dma_last.values()) if debug else []))
    st.close()
    return nc, P


def _swap_pairs(a, axis=-1):
    a = np.moveaxis(a, axis, -1)
    sh = a.shape
    b = a.reshape(sh[:-1] + (sh[-1] // 2, 2))[..., ::-1].reshape(sh)
    return np.moveaxis(b, -1, axis)


def _rope_tables(rot_dim):
    rows = T // 64
    row_ids = np.repeat(np.arange(rows, dtype=np.float32), 64)
    col_ids = np.tile(np.arange(64, dtype=np.float32), rows)
    n = rot_dim // 4
    freqs = (np.float32(10000.0) ** (-np.arange(n, dtype=np.float32) / np.float32(n))).astype(np.float32)
    ang = np.concatenate([row_ids[:, None] * freqs, col_ids[:, None] * freqs], axis=-1).astype(np.float32)
    cos = np.cos(ang).astype(np.float32)
    sin = np.sin(ang).astype(np.float32)
    ctab = np.ones((rot_dim, NT), np.float32)
    stab = np.zeros((rot_dim, NT), np.float32)
    ctab[:, 0:T] = np.repeat(cos.T, 2, axis=0)
    s2 = np.repeat(sin.T, 2, axis=0)
    s2[0::2] *= -1.0
    stab[:, 0:T] = s2
    return ctab, stab


_CONST = {}


def _consts():
    if _CONST:
        return _CONST
    cm, sm = _rope_tables(32)
    ropeM = np.zeros((2, 128, NT), np.float32)
    ropeM[0, 64:96] = cm
    ropeM[0, 0:32] = sm
    ropeM[1, 64:96] = sm
    cd, sd = _rope_tables(64)
    ropeD = np.zeros((2, 128, NT), np.float32)
    ropeD[0] = np.concatenate([cd, cd], axis=0)
    ropeD[1] = np.concatenate([sd, sd], axis=0)
    masks = np.zeros((128, 3, 128), np.float32)
    masks[:, 0, :] = 1.0
    masks[0:96, 1, 0:96] = 1.0
    masks[0:64, 2, 0:64] = 1.0
    masks[64:128, 2, 64:128] = 1.0
    _CONST.update(ident=np.eye(128, dtype=np.float32), masks=masks, ropeM=ropeM, ropeD=ropeD)
    return _CONST


def _pcol(v):
    return np.ascontiguousarray(v.reshape(-1, 128).T)


def _prep_layer(inp, l):
    f = lambda k: np.asarray(inp[k][l], dtype=np.float32)
    w_in = f("w_in")
    ext = np.empty((D, NEXT), np.float32)
    ext[:, O_RX:O_RX + 1024] = w_in[:, 0:1024]
    ext[:, O_RG:O_RG + 1024] = w_in[:, 1024:2048]
    ext[:, O_CQ:O_CQ + 384] = w_in[:, 2048:2432]
    ext[:, O_CKV:O_CKV + 256] = w_in[:, 2432:2688]
    kr = w_in[:, 2688:2720]
    ext[:, O_KR:O_KR + 32] = kr
    ext[:, O_KR + 32:O_KR + 64] = _swap_pairs(kr)
    for (src, dst) in ((2720, O_DQ), (3744, O_DK)):
        for h in range(8):
            blk = w_in[:, src + h * 128:src + (h + 1) * 128]
            ext[:, dst + h * 256:dst + h * 256 + 128] = blk
            ext[:, dst + h * 256 + 128:dst + (h + 1) * 256] = _swap_pairs(blk.reshape(D, 2, 64)).reshape(D, 128)
    ext[:, O_MG:O_MG + 3072] = w_in[:, 5792:8864]
    ext[:, O_DV:O_DV + 1024] = w_in[:, 4768:5792]
    uq = f("mla_w_uq").reshape(384, 16, 96)
    wuq = np.concatenate([uq, _swap_pairs(uq[:, :, 64:96])], axis=2).reshape(384, 2048)
    ukv = f("mla_w_ukv").reshape(256, 16, 128)
    wukvk = np.ascontiguousarray(ukv[:, :, 0:64]).reshape(256, 1024)
    wukvv = np.ascontiguousarray(ukv[:, :, 64:128]).reshape(256, 1024)
    lru = np.stack([f("lru_wa"), f("lru_wi")], axis=0)
    lru = np.ascontiguousarray(lru.transpose(3, 0, 1, 2, 4)).reshape(128, 32, 128)
    ffi = f("w_ffn_in").reshape(D, 2, 22, 128).transpose(0, 2, 1, 3).reshape(D, 2 * FFH)
    vecs = np.zeros((128, NV), np.float32)
    vecs[:, V_BMOD:V_BMOD + 48] = _pcol(f("b_mod"))
    vecs[:, V_N1G:V_N1G + 8] = _pcol(f("norm1_g"))
    vecs[:, V_N2G:V_N2G + 8] = _pcol(f("norm2_g"))
    cw = f("conv_w")
    for tap in range(4):
        vecs[:, V_CONVW + tap * 8:V_CONVW + tap * 8 + 8] = _pcol(cw[tap])
    vecs[:, V_CONVB:V_CONVB + 8] = _pcol(f("conv_b"))
    for (nm, vb) in (("lru_ba", V_BA), ("lru_bi", V_BI), ("lru_lambda", V_LAM)):
        a = f(nm)
        for r in range(2):
            vecs[:, vb + r * 8:vb + r * 8 + 8] = _pcol(a[r])
    vecs[:, V_QNG:V_QNG + 3] = _pcol(f("mla_qn_g"))
    vecs[:, V_KVNG:V_KVNG + 2] = _pcol(f("mla_kvn_g"))
    for (nm, vb) in (("mla_q_g", V_GQ), ("mla_k_g", V_GK)):
        g = f(nm)
        vecs[0:96, vb] = g
        vecs[64:96, vb + 1] = _swap_pairs(g[64:96])
    for (nm, vb) in (("diff_q_g", V_GDQ), ("diff_k_g", V_GDK)):
        g = f(nm)
        vecs[:, vb] = np.concatenate([g, g])
        vecs[:, vb + 1] = np.concatenate([_swap_pairs(g), _swap_pairs(g)])
    return {
        f"wmod{l}": f("w_mod"), f"win{l}": ext, f"wuq{l}": np.ascontiguousarray(wuq), f"wukvk{l}": wukvk, f"wukvv{l}": wukvv,
        f"lru{l}": lru, f"wbra{l}": f("w_br_a"), f"wbrb{l}": f("w_br_b"), f"wbrc{l}": f("w_br_c"), f"wout{l}": f("w_out"),
        f"wffi{l}": np.ascontiguousarray(ffi), f"wffo{l}": f("w_ffn_out"), f"vecs{l}": vecs,
        f"dlam{l}": np.ascontiguousarray(f("diff_lambda").reshape(1, 256)), f"subg{l}": np.ascontiguousarray(f("diff_subln_g").reshape(1, 128)),
    }


def _prep_core(inp, b):
    x = np.asarray(inp["x"][b], dtype=np.float32)
    ctx = np.asarray(inp["ctx"][b], dtype=np.float32)
    xT = np.ascontiguousarray(np.concatenate([x.T, ctx.T], axis=1))
    cv = np.stack([_pcol(np.asarray(inp["c"][b], np.float32)), _pcol(np.asarray(inp["c_ctx"], np.float32))], axis=2)
    return {"xT": xT, "cvec": np.ascontiguousarray(cv)}


_PROG = {}


def kernel(**inputs):
    n_cores = 8
    if "nc" not in _PROG:
        _PROG["nc"] = build(DEPTH, debug=False)[0]
    nc = _PROG["nc"]
    shared = dict(_consts())
    for l in range(DEPTH):
        shared.update(_prep_layer(inputs, l))
    in_maps = []
    for b in range(n_cores):
        m = dict(shared)
        m.update(_prep_core(inputs, b))
        in_maps.append(m)
    res = run_bass_kernel_spmd(nc, in_maps, core_ids=list(range(n_cores)))
    out = np.stack([np.asarray(r["outT"], dtype=np.float32).T for r in res.results], axis=0)
    return np.ascontiguousarray(out)
```
